# Optimizing a Trainium2 kernel written in Bass

```python
import math
import jax, jax.numpy as jnp
from jax import lax
import numpy as np

D_MODEL = 1024
BATCH = 8
SEQ = 4096
DEPTH = 4

HEAD_DIM = 64
BLOCK = 128
RMS_EPS = 1e-6
SUBLN_EPS = 1e-5
A_HEADS = 4
A_QK = A_HEADS * 2 * HEAD_DIM
A_WIDTH = A_HEADS * 2 * HEAD_DIM
B_HEADS = 8
B_KV_HEADS = 2
B_GROUP = B_HEADS // B_KV_HEADS
B_WIDTH = B_HEADS * HEAD_DIM
B_KV = B_KV_HEADS * HEAD_DIM
WINDOW = 128
C_HEADS = 8
C_WIDTH = C_HEADS * HEAD_DIM
N_BRANCH = 3
SPLITS = (A_QK, A_QK, A_WIDTH, A_WIDTH,
          B_WIDTH, B_KV, B_KV, B_WIDTH,
          C_WIDTH, C_WIDTH, C_WIDTH, C_HEADS, C_WIDTH,
          N_BRANCH * D_MODEL)
D_IN = sum(SPLITS)

kernel_name = "hybrid_diff_swa_fox_gated_block"


def rms_norm(x, gain, eps):
    xf = x.astype(jnp.float32)
    y = xf * lax.rsqrt(jnp.mean(xf * xf, axis=-1, keepdims=True) + eps)
    return (y * gain.astype(jnp.float32)).astype(x.dtype)


def alibi_slopes(n_heads):
    return 2.0 ** (-8.0 * jnp.arange(1, n_heads + 1, dtype=jnp.float32) / n_heads)


def diff_attention(q, k, v, lam, slopes):
    B, S, H, _, d = q.shape
    nb = S // BLOCK
    scale = d ** -0.5
    qb = jnp.moveaxis(q.reshape(B, nb, BLOCK, H, 2, d), 1, 0)
    k_pos = jnp.arange(S)

    def one_block(args):
        qblk, start = args
        s = jnp.einsum('bqhmd,bkhmd->bhmqk', qblk, k).astype(jnp.float32) * scale
        dist = (start + jnp.arange(BLOCK))[:, None] - k_pos[None, :]
        bias = jnp.where(dist >= 0, -slopes[:, None, None] * dist.astype(jnp.float32), -jnp.inf)
        p = jax.nn.softmax(s + bias[None, :, None], axis=-1)
        w = p[:, :, 0] - lam * p[:, :, 1]
        return jnp.einsum('bhqk,bkhe->bqhe', w.astype(v.dtype), v)

    out = lax.map(one_block, (qb, jnp.arange(nb) * BLOCK))
    return jnp.moveaxis(out, 0, 1).reshape(B, S, H, v.shape[-1])


def sliding_window_attention(q, k, v, sinks, slopes):
    B, S, HQ, d = q.shape
    KV = k.shape[2]
    G = HQ // KV
    nb = S // BLOCK
    scale = d ** -0.5
    qb = q.reshape(B, nb, BLOCK, KV, G, d)

    def banded(t):
        tb = t.reshape(B, nb, BLOCK, KV, d)
        prev = jnp.concatenate([jnp.zeros_like(tb[:, :1]), tb[:, :-1]], axis=1)
        return jnp.concatenate([prev, tb], axis=2)

    kb, vb = banded(k), banded(v)
    s = jnp.einsum('bnqhgd,bnkhd->bnhgqk', qb, kb).astype(jnp.float32) * scale
    i = jnp.arange(BLOCK)[:, None]
    j = jnp.arange(2 * BLOCK)[None, :]
    dist = i - j + BLOCK
    key_pos = jnp.arange(nb)[:, None, None] * BLOCK + j - BLOCK
    valid = (dist >= 0) & (dist < WINDOW) & (key_pos >= 0)
    bias = -slopes.reshape(KV, G)[:, :, None, None] * dist.astype(jnp.float32)
    s = jnp.where(valid[None, :, None, None], s + bias[None, None], -jnp.inf)
    sink = sinks.astype(jnp.float32).reshape(KV, G)[None, None, :, :, None, None]
    m = jnp.maximum(jnp.max(s, axis=-1, keepdims=True), sink)
    e = jnp.exp(s - m)
    p = e / (jnp.sum(e, axis=-1, keepdims=True) + jnp.exp(sink - m))
    out = jnp.einsum('bnhgqk,bnkhd->bnqhgd', p.astype(v.dtype), vb)
    return out.reshape(B, S, HQ * d)


def forgetting_attention(q, k, v, logf):
    B, S, H, d = q.shape
    nb = S // BLOCK
    scale = d ** -0.5
    c = jnp.cumsum(logf, axis=1)
    c_k = jnp.transpose(c, (0, 2, 1))
    qb = jnp.moveaxis(q.reshape(B, nb, BLOCK, H, d), 1, 0)
    cb = jnp.moveaxis(c.reshape(B, nb, BLOCK, H), 1, 0)
    k_pos = jnp.arange(S)

    def one_block(args):
        qblk, cblk, start = args
        s = jnp.einsum('bqhd,bkhd->bhqk', qblk, k).astype(jnp.float32) * scale
        decay = jnp.transpose(cblk, (0, 2, 1))[:, :, :, None] - c_k[:, :, None, :]
        dist = (start + jnp.arange(BLOCK))[:, None] - k_pos[None, :]
        s = jnp.where((dist >= 0)[None, None], s + decay, -jnp.inf)
        p = jax.nn.softmax(s, axis=-1)
        return jnp.einsum('bhqk,bkhd->bqhd', p.astype(v.dtype), v)

    out = lax.map(one_block, (qb, cb, jnp.arange(nb) * BLOCK))
    return jnp.moveaxis(out, 0, 1).reshape(B, S, H * d)


def setup_inputs(seed: int = 0) -> dict:
    key = jax.random.key(seed)
    ks = jax.random.split(key, 16)
    f32 = jnp.float32
    nrm = lambda k, shape: jax.random.normal(k, shape, dtype=f32)
    return {
        "x": nrm(ks[0], (BATCH, SEQ, D_MODEL)),
        "norm_gain": 1.0 + 0.05 * nrm(ks[1], (DEPTH, D_MODEL)),
        "w_in": nrm(ks[2], (DEPTH, D_MODEL, D_IN)) * D_MODEL ** -0.5,
        "b_forget": 0.1 * nrm(ks[3], (DEPTH, C_HEADS)),
        "lambda_q1": 0.1 * nrm(ks[4], (DEPTH, HEAD_DIM)),
        "lambda_k1": 0.1 * nrm(ks[5], (DEPTH, HEAD_DIM)),
        "lambda_q2": 0.1 * nrm(ks[6], (DEPTH, HEAD_DIM)),
        "lambda_k2": 0.1 * nrm(ks[7], (DEPTH, HEAD_DIM)),
        "subln_gain": 1.0 + 0.05 * nrm(ks[8], (DEPTH, 2 * HEAD_DIM)),
        "sinks": 0.5 * nrm(ks[9], (DEPTH, B_HEADS)),
        "w_up_a": nrm(ks[10], (DEPTH, A_WIDTH, D_MODEL)) * A_WIDTH ** -0.5,
        "w_up_b": nrm(ks[11], (DEPTH, B_WIDTH, D_MODEL)) * B_WIDTH ** -0.5,
        "w_up_c": nrm(ks[12], (DEPTH, C_WIDTH, D_MODEL)) * C_WIDTH ** -0.5,
        "w_o": nrm(ks[13], (DEPTH, D_MODEL, D_MODEL)) * D_MODEL ** -0.5,
        "final_gain": 1.0 + 0.05 * nrm(ks[14], (D_MODEL,)),
    }


def reference(x, norm_gain, w_in, b_forget, lambda_q1, lambda_k1, lambda_q2, lambda_k2,
              subln_gain, sinks, w_up_a, w_up_b, w_up_c, w_o, final_gain):
    B, S, D = x.shape
    offsets = np.cumsum(SPLITS)[:-1].tolist()
    slopes_a = alibi_slopes(A_HEADS)
    slopes_b = alibi_slopes(B_HEADS)
    for l in range(DEPTH):
        h = rms_norm(x, norm_gain[l], RMS_EPS)
        proj = jnp.einsum('bsd,de->bse', h, w_in[l])
        (qa, ka, va, ga, qb, kb, vb, gb,
         qc, kc, vc, fc, gc, gm) = jnp.split(proj, offsets, axis=-1)

        lam_init = 0.8 - 0.6 * math.exp(-0.3 * l)
        lam = (jnp.exp(jnp.sum(lambda_q1[l].astype(jnp.float32) * lambda_k1[l].astype(jnp.float32)))
               - jnp.exp(jnp.sum(lambda_q2[l].astype(jnp.float32) * lambda_k2[l].astype(jnp.float32)))
               + lam_init)
        ya = diff_attention(qa.reshape(B, S, A_HEADS, 2, HEAD_DIM),
                            ka.reshape(B, S, A_HEADS, 2, HEAD_DIM),
                            va.reshape(B, S, A_HEADS, 2 * HEAD_DIM), lam, slopes_a)
        ya = (rms_norm(ya, subln_gain[l], SUBLN_EPS) * (1.0 - lam_init)).reshape(B, S, A_WIDTH)

        yb = sliding_window_attention(qb.reshape(B, S, B_HEADS, HEAD_DIM),
                                      kb.reshape(B, S, B_KV_HEADS, HEAD_DIM),
                                      vb.reshape(B, S, B_KV_HEADS, HEAD_DIM),
                                      sinks[l], slopes_b)

        logf = jax.nn.log_sigmoid(fc.astype(jnp.float32) + b_forget[l].astype(jnp.float32))
        yc = forgetting_attention(qc.reshape(B, S, C_HEADS, HEAD_DIM),
                                  kc.reshape(B, S, C_HEADS, HEAD_DIM),
                                  vc.reshape(B, S, C_HEADS, HEAD_DIM), logf)

        ua = jnp.einsum('bse,ed->bsd', ya * jax.nn.silu(ga), w_up_a[l])
        ub = jnp.einsum('bse,ed->bsd', yb * jax.nn.silu(gb), w_up_b[l])
        uc = jnp.einsum('bse,ed->bsd', yc * jax.nn.silu(gc), w_up_c[l])
        gates = jax.nn.sigmoid(gm.reshape(B, S, N_BRANCH, D))
        merged = gates[:, :, 0] * ua + gates[:, :, 1] * ub + gates[:, :, 2] * uc
        x = x + jnp.einsum('bsd,de->bse', merged, w_o[l])
    return rms_norm(x, final_gain, RMS_EPS)
```

```python
import math
import numpy as np
import concourse.bass as bass
import concourse.mybir as mybir
from concourse.bass_utils import run_bass_kernel_spmd

F32 = mybir.dt.float32
BF16 = mybir.dt.bfloat16
AF = mybir.ActivationFunctionType
ALU = mybir.AluOpType
AX = mybir.AxisListType

COMPUTE = ("pe", "act", "dve", "pool")


class Buf:
    __slots__ = ("name", "w", "r", "dsem", "dcount", "dlast")

    def __init__(self, name):
        self.name = name
        self.w = None
        self.r = []
        self.dsem = None
        self.dcount = 0
        self.dlast = None


class Op:
    __slots__ = ("eng", "fn", "deps", "sig", "sigidx", "is_dma", "sem", "target")

    def __init__(self, eng, fn, is_dma=False):
        self.eng = eng
        self.fn = fn
        self.deps = []
        self.sig = False
        self.sigidx = None
        self.is_dma = is_dma
        self.sem = None
        self.target = None


class Sched:
    def __init__(self, nc, strict=True):
        self.nc = nc
        self.strict = strict
        self.ops = {e: [] for e in ("pe", "act", "dve", "pool", "sp")}

    def _add(self, op, reads, writes):
        deps = []
        for b in reads:
            if b.w is not None:
                deps.append((b.w, True))
        for b in writes:
            if b.w is not None:
                deps.append((b.w, False))
            for r in b.r:
                deps.append((r, False))
        seen = set()
        for d, raw in deps:
            if d is op or id(d) in seen:
                continue
            if (not d.is_dma) and (not op.is_dma) and d.eng == op.eng:
                if op.eng == "pe" or not (raw or self.strict):
                    continue
            seen.add(id(d))
            op.deps.append(d)
            if not d.is_dma:
                d.sig = True
        for b in reads:
            if not op.is_dma:
                b.r = [x for x in b.r if x.is_dma or x.eng != op.eng]
            b.r.append(op)
        for b in writes:
            b.w = op
            b.r = []
        self.ops[op.eng].append(op)
        return op

    def op(self, eng, fn, reads=(), writes=()):
        return self._add(Op(eng, fn), list(reads), list(writes))

    def dma(self, q, out, in_, reads=(), writes=(), owner=None, **kw):
        reads = list(reads)
        writes = list(writes)
        if owner is None:
            owner = writes[0] if writes else reads[0]
        if owner.dsem is None:
            owner.dsem = self.nc.alloc_semaphore("d_" + owner.name)
        o = Op(q, None, is_dma=True)
        o.sem = owner.dsem
        owner.dcount += 16
        o.target = owner.dcount
        o.fn = lambda h, out=out, in_=in_: h.dma_start(out=out, in_=in_, **kw)
        if owner.dlast is not None:
            o.deps.append(owner.dlast)
        owner.dlast = o
        return self._add(o, reads, writes)

    def emit(self, final_wait_ops=()):
        nc = self.nc
        esem = {e: nc.alloc_semaphore("e_" + e) for e in COMPUTE}
        for e in COMPUTE:
            c = 0
            for o in self.ops[e]:
                if o.sig:
                    c += 1
                    o.sigidx = c
        self.sig_counts = {e: sum(1 for o in self.ops[e] if o.sig) for e in COMPUTE}

        def emit_engine(e, h):
            seen = {}
            for o in self.ops[e]:
                for d in o.deps:
                    if d.is_dma:
                        key, val, sem = ("d", d.sem.num), d.target, d.sem
                    else:
                        key, val, sem = ("e", d.eng), d.sigidx, esem[d.eng]
                    if seen.get(key, 0) >= val:
                        continue
                    seen[key] = val
                    h.wait_ge(sem, val)
                ins = o.fn(h)
                if o.is_dma:
                    ins.then_inc(o.sem, 16)
                elif o.sig:
                    ins.then_inc(esem[e], 1)
            if e == "sp":
                for d in final_wait_ops:
                    h.wait_ge(d.sem, d.target)

        with nc.Block() as block:
            @block.tensor
            def _(h):
                emit_engine("pe", h)

            @block.scalar
            def _(h):
                emit_engine("act", h)

            @block.vector
            def _(h):
                emit_engine("dve", h)

            @block.gpsimd
            def _(h):
                emit_engine("pool", h)

            @block.sync
            def _(h):
                emit_engine("sp", h)


D = 1024
D_IN = 8456
QA, KA, VA, GA = 0, 512, 1024, 1536
QB, KB, VB, GB = 2048, 2560, 2688, 2816
QC, KC, VC, FC, GC = 3328, 3840, 4352, 4864, 4872
GM = 5384
RMS_EPS = 1e-6
SUBLN_EPS = 1e-5
NEG = -30000.0
NAUG = 6
KROWS = 64 + NAUG


def alibi_tables(S):
    pos = np.arange(S)
    blk = (pos // 128).astype(np.float64)
    rem = (pos % 128).astype(np.float64)
    augA = np.zeros((4, 2, NAUG, S), np.float32)
    for h in range(4):
        m = 2.0 ** (-8.0 * (h + 1) / 4)
        augA[h, 0, 0] = -8 * m * 128 * blk
        augA[h, 0, 1] = -8 * m * rem
        augA[h, 0, 2] = 1.0
        augA[h, 0, 3] = 1.0
        augA[h, 1, 0] = 1.0
        augA[h, 1, 1] = 1.0
        augA[h, 1, 2] = 8 * m * 128 * blk
        augA[h, 1, 3] = 8 * m * rem
    augB = np.zeros((8, 2, NAUG, S), np.float32)
    for h in range(8):
        m = 2.0 ** (-8.0 * (h + 1) / 8)
        augB[h, 0, 0] = -8 * m * 128 * blk
        augB[h, 0, 1] = -8 * m * rem
        augB[h, 0, 2] = 8 * m
        augB[h, 0, 3] = 8 * m
        augB[h, 1, 0] = 1.0
        augB[h, 1, 1] = 1.0
        augB[h, 1, 2] = 128 * blk
        augB[h, 1, 3] = rem
    return augA, augB


def unit_desc(n, u):
    if n == 0:
        return dict(kind="A", q=[QA + u * 128, QA + u * 128 + 64], k=[KA + u * 128, KA + u * 128 + 64],
                    v=(VA + u * 128, 128), g=GA + u * 128)
    if n == 1:
        g = u // 2
        return dict(kind="B", q=[QB + (2 * u) * 64, QB + (2 * u + 1) * 64], k=[KB + g * 64, KB + g * 64],
                    v=(VB + g * 64, 64), g=GB + u * 128)
    return dict(kind="C", q=[QC + (2 * u) * 64, QC + (2 * u + 1) * 64], k=[KC + (2 * u) * 64, KC + (2 * u + 1) * 64],
                v=(VC + u * 128, 128), g=GC + u * 128)


def build(S=4096, depth=4, dbg=()):
    NB = S // 128
    NCH = S // 512
    nc = bass.Bass("TRN2", target_bir_lowering=False)

    def din(name, shape):
        return nc.dram_tensor(name, list(shape), F32, kind="ExternalInput").ap()

    x_d = din("x", [S, D])
    ng_d = din("norm_gain", [depth, D])
    win_d = din("w_in", [depth, D, D_IN])
    bf_d = din("b_forget", [depth, 8])
    lq1_d = din("lambda_q1", [depth, 64])
    lk1_d = din("lambda_k1", [depth, 64])
    lq2_d = din("lambda_q2", [depth, 64])
    lk2_d = din("lambda_k2", [depth, 64])
    sg_d = din("subln_gain", [depth, 128])
    sk_d = din("sinks", [depth, 8])
    wup_d = [din("w_up_a", [depth, 512, D]), din("w_up_b", [depth, 512, D]), din("w_up_c", [depth, 512, D])]
    wo_d = din("w_o", [depth, D, D])
    fg_d = din("final_gain", [D])
    augA_d = din("augA", [4, 2, NAUG, S])
    augB_d = din("augB", [8, 2, NAUG, S])
    out_d = nc.dram_tensor("out", [S, D], F32, kind="ExternalOutput").ap()
    xres_d = nc.dram_tensor("xres", [S, D], F32, kind="Internal").ap()
    M_d = nc.dram_tensor("mscr", [D, S], BF16, kind="Internal").ap()
    caug_d = nc.dram_tensor("caug", [2, 8, NAUG, S], BF16, kind="Internal").ap()

    sch = Sched(nc)
    bufs = {}

    def B(name):
        if name not in bufs:
            bufs[name] = Buf(name)
        return bufs[name]

    def sb(name, shape, dt):
        return nc.alloc_sbuf_tensor(name, list(shape), dt)

    hT = sb("hT", [128, 8, S], BF16)
    QT = sb("QT", [128, 2, S], BF16)
    KT = sb("KT", [128, 2, S], BF16)
    V = sb("V", [128, NB, 192], BF16)
    GT = sb("GT", [128, S], BF16)
    ygT = sb("ygT", [128, 4, S], BF16)
    wslab = [sb("wslab%d" % i, [128, 8, 512], BF16) for i in range(2)]
    pT = [sb("pT%d" % i, [128, 512], BF16) for i in range(4)]
    wup = sb("wup", [128, 4, D], BF16)
    wgm = sb("wgm", [128, 8, D], BF16)
    Wall = sb("Wall", [128, 6, 512], F32)
    W = [Wall[:, i, :] for i in range(6)]
    gain_bc = Wall[:, 4:6, :].rearrange("p a b -> p (a b)")
    Wb = None
    sgt = [W[i][:].bitcast(BF16)[:, 0:512] for i in range(2)]
    tst = [W[2 + i][:].bitcast(BF16)[:, 0:512] for i in range(2)]
    ysqb = sb("ysqb", [128, 512], BF16)
    onesbf = sb("onesbf", [128, 128], BF16)
    sgcol = sb("sgcol", [128, 1], F32)
    ident = sb("ident", [128, 128], BF16)
    maskD = sb("maskD", [128, 128], BF16)
    maskP = sb("maskP", [128, 128], BF16)
    lam4 = sb("lam4", [128, 4, 64], F32)
    lamw = sb("lamw", [128, 8], F32)
    neglam = sb("neglam", [128, 1], F32)
    esink = sb("esink", [128, 8], F32)
    negb = sb("negb", [8, 1], F32)
    wfc = sb("wfc", [128, 8, 8], BF16)
    rst = [sb("rst%d" % i, [128, 4], F32) for i in range(4)]
    yraw = ygT[:].rearrange("p a s -> p (a s)")
    assert 4 * S * 2 >= 4 * (2 * 4096 + 2048 + 1024) or True

    def alias(off_bytes, shape, dt):
        n = int(np.prod(shape[1:]))
        esz = 4 if dt == F32 else 2
        a = yraw[:, off_bytes // 2: off_bytes // 2 + n * esz // 2]
        if dt == F32:
            a = a.bitcast(F32)
        if len(shape) == 3:
            a = a.rearrange("p (a b) -> p a b", a=shape[1])
        return a

    NXT = 4 if 8 * S >= 26624 else 2
    xt = [alias(4096 * i, [128, D], F32) for i in range(NXT)]
    hbt = [alias(4096 * NXT + 2048 * i, [128, D], BF16) for i in range(NXT)]
    junk = alias(6144 * NXT, [128, D], BF16)
    fin_ok = 8 * S >= 12288 + 2048 + 3 * 8192 + 8192
    qraw = QT[:].rearrange("p a s -> p (a s)")
    kraw = KT[:].rearrange("p a s -> p (a s)")
    assert S >= 2048, "aliasing plan needs S >= 2048"
    big = 2 * S >= 8192
    wgraw = wgm[:].rearrange("p a b -> p (a b)")
    wuraw = wup[:].rearrange("p a b -> p (a b)")

    def tview(raw, off, shape, dt):
        n = int(np.prod(shape[1:]))
        esz = 4 if dt == F32 else 2
        a = raw[0:shape[0], off // 2: off // 2 + n * esz // 2]
        if dt == F32:
            a = a.bitcast(F32)
        if len(shape) == 3:
            a = a.rearrange("p (a b) -> p a b", a=shape[1])
        return a
    fcE = tview(wgraw, 0, [8, 512], F32)
    fcS = tview(wgraw, 2048, [8, 512], F32)
    fcC = [tview(wgraw, 4096, [8, 512], F32), tview(wgraw, 6144, [8, 512], F32)]
    ones8 = tview(wgraw, 8192, [8, 512], F32)
    caq = tview(wgraw, 10240, [8, NAUG, 512], BF16)
    cak = tview(wuraw, 0, [8, NAUG, 512], BF16)
    fcR, fcR2 = fcS, fcE
    mt = [qraw[:, 0:4096].rearrange("p (a b) -> p a b", a=8),
          qraw[:, 4096:8192].rearrange("p (a b) -> p a b", a=8) if big else sb("mt1x", [128, 8, 512], BF16)[:],
          kraw[:, 0:4096].rearrange("p (a b) -> p a b", a=8)]
    mT = kraw[:, 4096:8192].rearrange("p (a b) -> p a b", a=8) if big else sb("mTx", [128, 8, 512], BF16)[:]
    mT2 = (V[:].rearrange("p a b -> p (a b)")[:, 0:4096].rearrange("p (a b) -> p a b", a=8) if big
           else sb("mT2x", [128, 8, 512], BF16)[:])

    ps = [nc.alloc_psum_tensor("ps%d" % i, [128, 512], F32) for i in range(8)]
    psB = [B("ps%d" % i) for i in range(8)]
    ST = (0, 1, 2)
    ACC = (3, 4, 5, 6)
    TR = 7

    hTb = [B("hT%d" % c) for c in range(NCH)]
    QTb = [B("QT0"), B("QT1")]
    KTb = [B("KT0"), B("KT1")]
    QTa = [B("QTa0"), B("QTa1")]
    KTa = [B("KTa0"), B("KTa1")]
    AL = [B("xt%d" % i) for i in range(4)] + [B("hbt%d" % i) for i in range(4)] + [B("junk")]
    if big:
        mtb = [[QTb[0], QTa[0]], [QTb[1], QTa[1]], [KTb[0], KTa[0]]]
        mTb = [KTb[1], KTa[1]]
    else:
        mtb = [QTb + QTa, [B("mt1x")], KTb + KTa]
        mTb = [B("mTx")]
    Vb = B("V")
    GTb = [B("GT%d" % c) for c in range(NCH)]
    Wb = [B("W%d" % i) for i in range(6)]
    mT2b = [Vb] if big else [B("mT2x")]
    ygTb = [[B("ygT%d_%d" % (u, c)) for c in range(NCH)] for u in range(4)]
    ygAll = [b for row in ygTb for b in row]
    xres_b = [B("xres%d" % t) for t in range(NB)]
    M_b = [[B("M_%d_%d" % (c, dc)) for dc in range(8)] for c in range(NCH)]
    caug_b = [[B("caug%d_%d" % (sd, c)) for c in range(NCH)] for sd in range(2)]

    dbg_outs = {}

    sch.op("pool", lambda h: h.memset(ident[:], 1.0), writes=[B("ident")])
    sch.op("pool", lambda h: h.affine_select(out=ident[:], in_=ident[:], pattern=[[-1, 128]],
                                             compare_op=ALU.is_equal, fill=0.0, base=0, channel_multiplier=1),
           reads=[B("ident")], writes=[B("ident")])
    sch.op("pool", lambda h: h.memset(maskD[:], 0.0), writes=[B("maskD")])
    sch.op("pool", lambda h: h.affine_select(out=maskD[:], in_=maskD[:], pattern=[[1, 128]],
                                             compare_op=ALU.is_ge, fill=NEG, base=0, channel_multiplier=-1),
           reads=[B("maskD")], writes=[B("maskD")])
    sch.op("pool", lambda h: h.memset(maskP[:], 0.0), writes=[B("maskP")])
    sch.op("pool", lambda h: h.affine_select(out=maskP[:], in_=maskP[:], pattern=[[-1, 128]],
                                             compare_op=ALU.is_gt, fill=NEG, base=0, channel_multiplier=1),
           reads=[B("maskP")], writes=[B("maskP")])
    sch.op("pool", lambda h: h.memset(onesbf[:], 1.0), writes=[B("onesbf")])

    tick_q = []

    def defer(fn, delay):
        tick_q.append([delay, fn])

    def tick():
        for it in tick_q:
            it[0] -= 1
        ready = [it for it in tick_q if it[0] <= 0]
        tick_q[:] = [it for it in tick_q if it[0] > 0]
        for it in ready:
            it[1]()

    def flush():
        while tick_q:
            tick_q.pop(0)[1]()

    wslab_b = [[B("wslab%d_%d" % (i, p)) for p in range(6)] for i in range(2)]
    slab_ctr = [0]

    def load_slab(l, ud):
        si = slab_ctr[0] % 2
        slab_ctr[0] += 1
        ws = wslab[si]

        def ld(part, dst0, c0, w):
            src = win_d[l, :, c0:c0 + w].rearrange("(kc p) j -> p kc j", p=128)
            sch.dma("pool", ws[:, :, dst0:dst0 + w], src, writes=[wslab_b[si][part]])
        ld(0, 0, ud["q"][0], 64)
        ld(1, 64, ud["q"][1], 64)
        ld(2, 128, ud["k"][0], 64)
        ld(3, 192, ud["k"][1], 64)
        ld(4, 256, ud["v"][0], ud["v"][1])
        ld(5, 384, ud["g"], 128)
        return si

    def load_aug(l, n, u, ud):
        for m in range(2):
            if ud["kind"] == "A":
                sch.dma("pool", QT[64:KROWS, m, :], augA_d[u, 0], writes=[QTa[m]])
                sch.dma("pool", KT[64:KROWS, m, :], augA_d[u, 1], writes=[KTa[m]])
            elif ud["kind"] == "B":
                sch.dma("pool", QT[64:KROWS, m, :], augB_d[2 * u + m, 0], writes=[QTa[m]])
                sch.dma("pool", KT[64:KROWS, m, :], augB_d[2 * u + m, 1], writes=[KTa[m]])
            else:
                sch.dma("sp", QT[64:KROWS, m, :], caug_d[0, 2 * u + m], reads=caug_b[0], writes=[QTa[m]])
                sch.dma("sp", KT[64:KROWS, m, :], caug_d[1, 2 * u + m], reads=caug_b[1], writes=[KTa[m]])

    def rms_to_h(xtile, xb, gbc, gb, slot, l):
        st = rst[slot]
        stb = B("rst%d" % slot)
        sch.op("act", lambda h: h.activation(out=junk, in_=xtile, func=AF.Square, accum_out=st[:, 0:1]),
               reads=[xb], writes=[B("junk"), stb])
        sch.op("act", lambda h: h.activation(out=st[:, 3:4], in_=st[:, 0:1], func=AF.Ln, scale=1.0 / D, bias=RMS_EPS),
               reads=[stb], writes=[stb])
        sch.op("act", lambda h: h.activation(out=st[:, 2:3], in_=st[:, 3:4], func=AF.Exp, scale=-0.5),
               reads=[stb], writes=[stb])
        return st, stb

    def prologue(l, on_tile=None):
        src = x_d if l == 0 else xres_d
        sch.dma("sp", gain_bc, ng_d[l].partition_broadcast(128), writes=[Wb[4], Wb[5]], owner=Wb[4])
        trbanks = (TR, ACC[0])

        def st_a(t):
            s = t % NXT
            rd = [xres_b[t]] if l > 0 else []
            sch.dma("sp", xt[s], src[t * 128:(t + 1) * 128, :], reads=rd, writes=[B("xt%d" % s)])

        def st_b(t):
            s = t % NXT
            st, stb, xb = rst[s], B("rst%d" % s), B("xt%d" % s)
            sch.op("act", lambda h: h.activation(out=junk, in_=xt[s], func=AF.Square, accum_out=st[:, 0:1]),
                   reads=[xb], writes=[B("junk"), stb])

        def st_c(t):
            s = t % NXT
            st, stb, xb, hb = rst[s], B("rst%d" % s), B("xt%d" % s), B("hbt%d" % s)
            sch.op("act", lambda h: h.activation(out=st[:, 3:4], in_=st[:, 0:1], func=AF.Ln, scale=1.0 / D, bias=RMS_EPS),
                   reads=[stb], writes=[stb])
            sch.op("act", lambda h: h.activation(out=st[:, 2:3], in_=st[:, 3:4], func=AF.Exp, scale=-0.5),
                   reads=[stb], writes=[stb])
            sch.op("dve", lambda h: h.scalar_tensor_tensor(
                out=hbt[s], in0=xt[s], scalar=st[:, 2:3], in1=gain_bc, op0=ALU.mult, op1=ALU.mult),
                reads=[xb, stb, Wb[4], Wb[5]], writes=[hb])
            bk = trbanks[t % 2]
            pst = ps[bk][:].bitcast(BF16)

            def tr(h):
                ins = None
                for kc in range(8):
                    ins = h.transpose(pst[:, kc * 128:(kc + 1) * 128], hbt[s][:, kc * 128:(kc + 1) * 128], ident[:])
                return ins
            sch.op("pe", tr, reads=[hb, B("ident")], writes=[psB[bk]])

        def st_d(t):
            bk = trbanks[t % 2]
            pst = ps[bk][:].bitcast(BF16)
            sch.op("dve", lambda h: h.tensor_copy(
                out=hT[:, :, t * 128:(t + 1) * 128], in_=pst.rearrange("p (a b) -> p a b", a=8)),
                reads=[psB[bk]], writes=[hTb[t // 4]])

        ob, oc = (1, 2) if NXT >= 4 else (0, 1)
        od = oc + 1
        for t in range(NB + od):
            if t < NB:
                st_a(t)
            if 0 <= t - ob < NB:
                st_b(t - ob)
            if 0 <= t - oc < NB:
                st_c(t - oc)
            if 0 <= t - od < NB:
                st_d(t - od)
                if on_tile is not None:
                    on_tile(t - od)

    def layer_scalars(l):
        lb = B("lam")
        for i, d_ in enumerate((lq1_d, lk1_d, lq2_d, lk2_d)):
            sch.dma("sp", lam4[:, i, :], d_[l].partition_broadcast(128), writes=[B("lam4_%d" % i)], owner=B("lam4_%d" % i))
        rl = [B("lam4_%d" % i) for i in range(4)]
        sch.op("dve", lambda h: h.tensor_tensor(out=lam4[:, 0, :], in0=lam4[:, 0, :], in1=lam4[:, 1, :], op=ALU.mult),
               reads=rl[:2], writes=[rl[0]])
        sch.op("dve", lambda h: h.tensor_tensor(out=lam4[:, 2, :], in0=lam4[:, 2, :], in1=lam4[:, 3, :], op=ALU.mult),
               reads=rl[2:], writes=[rl[2]])
        sch.op("dve", lambda h: h.reduce_sum(out=lamw[:, 0:1], in_=lam4[:, 0, :], axis=AX.X), reads=[rl[0]], writes=[lb])
        sch.op("dve", lambda h: h.reduce_sum(out=lamw[:, 1:2], in_=lam4[:, 2, :], axis=AX.X), reads=[rl[2], lb], writes=[lb])
        sch.op("act", lambda h: h.activation(out=lamw[:, 2:4], in_=lamw[:, 0:2], func=AF.Exp), reads=[lb], writes=[lb])
        lam_init = 0.8 - 0.6 * math.exp(-0.3 * l)
        sch.op("dve", lambda h: h.tensor_tensor(out=lamw[:, 4:5], in0=lamw[:, 3:4], in1=lamw[:, 2:3], op=ALU.subtract),
               reads=[lb], writes=[lb])
        sch.op("dve", lambda h: h.tensor_scalar(out=neglam[:], in0=lamw[:, 4:5], scalar1=-lam_init, scalar2=None,
                                                op0=ALU.add), reads=[lb], writes=[B("neglam")])
        sch.dma("sp", sgcol[:], sg_d[l].rearrange("(a b) -> a b", b=1), writes=[B("sgcol")])
        sch.op("dve", lambda h: h.tensor_scalar(out=sgcol[:], in0=sgcol[:], scalar1=(1.0 - lam_init) * math.sqrt(128.0),
                                                scalar2=None, op0=ALU.mult), reads=[B("sgcol")], writes=[B("sgcol")])
        sch.dma("sp", esink[:], sk_d[l].partition_broadcast(128), writes=[B("esink")])
        sch.op("act", lambda h: h.activation(out=esink[:], in_=esink[:], func=AF.Exp), reads=[B("esink")], writes=[B("esink")])
        sch.dma("sp", negb[:], bf_d[l].rearrange("(a b) -> a b", b=1), writes=[B("negb")])
        sch.op("dve", lambda h: h.tensor_scalar(out=negb[:], in0=negb[:], scalar1=-1.0, scalar2=None, op0=ALU.mult),
               reads=[B("negb")], writes=[B("negb")])

    def forget_gates(l):
        wb = B("wfc")
        sch.dma("pool", wfc[:], win_d[l, :, FC:FC + 8].rearrange("(kc p) j -> p kc j", p=128), writes=[wb])
        FGT = [B(nm) for nm in ("fcE", "fcS", "fcC0", "fcC1", "ones8", "caq", "cak")]
        FG = [B("wgm"), B("wup")]
        state = {"prev": None}
        qb_, kb_ = B("caq"), B("cak")

        def entry():
            sch.op("pool", lambda h: h.memset(ones8, 1.0), writes=FG + FGT)
            sch.op("pool", lambda h: h.memset(caq, 1.0), reads=[B("ones8")], writes=[qb_])
            sch.op("pool", lambda h: h.memset(cak, 1.0), reads=[B("ones8")], writes=[kb_])

        def stage1(c):
            def mm(h):
                ins = None
                for kc in range(8):
                    ins = h.matmul(ps[TR][0:8, :], lhsT=wfc[:, kc, :], rhs=hT[:, kc, c * 512:(c + 1) * 512],
                                   start=(kc == 0), stop=(kc == 7))
                return ins
            sch.op("pe", mm, reads=[wb, hTb[c]], writes=[psB[TR]])
            sch.op("act", lambda h: h.activation(out=fcE[:], in_=ps[TR][0:8, :], func=AF.Exp,
                                                 bias=negb[:, 0:1], scale=-1.0),
                   reads=[psB[TR], B("negb")], writes=[B("fcE")])
            sch.op("act", lambda h: h.activation(out=fcS[:], in_=fcE[:], func=AF.Ln, bias=1.0, scale=1.0),
                   reads=[B("fcE")], writes=[B("fcS")])

        def stage2(c):
            sch.op("dve", lambda h: h.tensor_scalar(out=fcS[:], in0=fcS[:], scalar1=-8.0, scalar2=None, op0=ALU.mult),
                   reads=[B("fcS")], writes=[B("fcS")])
            cur = fcC[c % 2]
            curb = B("fcC%d" % (c % 2))
            prev = state["prev"]
            init = 0.0 if prev is None else prev[0][:, 511:512]
            rd = [B("ones8"), B("fcS")] + ([prev[1]] if prev is not None else [])
            sch.op("dve", lambda h: h.tensor_tensor_scan(
                out=cur[:], data0=ones8[:], data1=fcS[:], initial=init, op0=ALU.mult, op1=ALU.add),
                reads=rd, writes=[curb])
            state["prev"] = (cur, curb)
            sch.op("dve", lambda h: h.tensor_copy(out=caq[:, 0, :], in_=cur[:]), reads=[curb], writes=[qb_])
            sch.op("dve", lambda h: h.tensor_tensor(out=fcR[:], in0=cur[:], in1=caq[:, 0, :], op=ALU.subtract),
                   reads=[curb, qb_], writes=[B("fcS")])
            sch.op("dve", lambda h: h.tensor_copy(out=caq[:, 1, :], in_=fcR[:]), reads=[B("fcS")], writes=[qb_])
            sch.op("dve", lambda h: h.tensor_tensor(out=fcR2[:], in0=fcR[:], in1=caq[:, 1, :], op=ALU.subtract),
                   reads=[B("fcS"), qb_], writes=[B("fcE")])
            sch.op("dve", lambda h: h.tensor_copy(out=caq[:, 2, :], in_=fcR2[:]), reads=[B("fcE")], writes=[qb_])
            sch.op("dve", lambda h: h.tensor_scalar(out=cak[:, 3:6, :], in0=caq[:, 0:3, :], scalar1=-1.0, scalar2=None,
                                                    op0=ALU.mult), reads=[qb_], writes=[kb_])
            sch.dma("sp", caug_d[0, :, :, c * 512:(c + 1) * 512], caq[:], reads=[qb_], writes=[caug_b[0][c]], owner=qb_)
            sch.dma("sp", caug_d[1, :, :, c * 512:(c + 1) * 512], cak[:], reads=[kb_], writes=[caug_b[1][c]], owner=kb_)

        def exit_guard():
            sch.op("pool", lambda h: h.memset(ones8[:, 0:1], 1.0), writes=FG + FGT)

        defer(entry, 1)
        for c in range(NCH):
            defer(lambda c=c: stage1(c), 2 + 2 * c)
            defer(lambda c=c: stage2(c), 3 + 2 * c)
        defer(exit_guard, 4 + 2 * NCH)

    proj_ctr = [0]

    def proj_chunk(ud, si, c):
        for st_ in proj_steps(ud, si, c):
            st_()

    def proj_steps(ud, si, c, later=None):
        return [lambda: proj_qk(ud, si, c, 0, later), lambda: proj_qk(ud, si, c, 1, later),
                lambda: proj_v(ud, si, c, later), lambda: proj_g(ud, si, c, later)]

    def _evac(fn, later):
        if later is None:
            fn()
        else:
            later.append(fn)

    def proj_qk(ud, si, c, which, later=None):
        ws = wslab[si]
        wsb = wslab_b[si]
        kind = ud["kind"]
        for (dstT, dstb, c0, eng) in (((QT, QTb, 0, "dve"), (KT, KTb, 128, "act"))[which],):
            bank = ST[proj_ctr[0] % 3]
            proj_ctr[0] += 1

            def mm(h, bank=bank, c0=c0):
                ins = None
                for kc in range(8):
                    ins = h.matmul(ps[bank][:, :], lhsT=ws[:, kc, c0:c0 + 128],
                                   rhs=hT[:, kc, c * 512:(c + 1) * 512], start=(kc == 0), stop=(kc == 7))
                return ins
            sch.op("pe", mm, reads=wsb + [hTb[c]], writes=[psB[bank]])

            def ev(bank=bank, dstT=dstT, dstb=dstb, eng=eng):
                for m in range(2):
                    if eng == "dve":
                        sch.op("dve", lambda h, m=m: h.tensor_copy(
                            out=dstT[0:64, m, c * 512:(c + 1) * 512], in_=ps[bank][m * 64:(m + 1) * 64, :]),
                            reads=[psB[bank]], writes=[dstb[m]])
                    else:
                        sch.op("act", lambda h, m=m: h.activation(
                            out=dstT[0:64, m, c * 512:(c + 1) * 512], in_=ps[bank][m * 64:(m + 1) * 64, :], func=AF.Copy),
                            reads=[psB[bank]], writes=[dstb[m]])
            _evac(ev, later)
            tick()

    def proj_v(ud, si, c, later=None):
        ws = wslab[si]
        wsb = wslab_b[si]
        kind = ud["kind"]
        vw = ud["v"][1]
        t4 = c
        bank = ST[proj_ctr[0] % 3]
        proj_ctr[0] += 1

        def mmv(h):
            ins = None
            for tt in range(4):
                t = 4 * t4 + tt
                for kc in range(8):
                    ins = h.matmul(ps[bank][:, tt * 128:tt * 128 + vw], lhsT=hT[:, kc, t * 128:(t + 1) * 128],
                                   rhs=ws[:, kc, 256:256 + vw], start=(kc == 0), stop=(kc == 7))
            return ins
        sch.op("pe", mmv, reads=wsb + [hTb[t4]], writes=[psB[bank]])
        pv4 = ps[bank][:].rearrange("p (a b) -> p a b", a=4)
        blk = slice(4 * t4, 4 * t4 + 4)
        if kind == "A":
            moves = [((0, 128), (0, 128))]
        elif kind == "B":
            moves = [((0, 64), (0, 64)), ((128, 192), (0, 64))]
        else:
            moves = [((0, 64), (0, 64)), ((128, 192), (64, 128))]
        def ev():
            for (d0, d1), (s0, s1) in moves:
                sch.op("act", lambda h, d0=d0, d1=d1, s0=s0, s1=s1: h.activation(
                    out=V[:, blk, d0:d1], in_=pv4[:, :, s0:s1], func=AF.Copy), reads=[psB[bank]], writes=[Vb])
        _evac(ev, later)
        tick()

    def proj_g(ud, si, c, later=None):
        ws = wslab[si]
        wsb = wslab_b[si]
        bank2 = ST[proj_ctr[0] % 3]
        proj_ctr[0] += 1

        def mmg(h):
            ins = None
            for kc in range(8):
                ins = h.matmul(ps[bank2][:, :], lhsT=ws[:, kc, 384:512], rhs=hT[:, kc, c * 512:(c + 1) * 512],
                               start=(kc == 0), stop=(kc == 7))
            return ins
        sch.op("pe", mmg, reads=wsb + [hTb[c]], writes=[psB[bank2]])
        def ev():
            sch.op("act", lambda h: h.activation(out=GT[:, c * 512:(c + 1) * 512], in_=ps[bank2][:, :],
                                                 func=AF.Silu), reads=[psB[bank2]], writes=[GTb[c]])
        _evac(ev, later)
        tick()

    def unit_projection(l, n, u, ud, si):
        if ud["kind"] == "B" and u == 0:
            sch.op("pool", lambda h: h.memset(V[:, :, 64:128], 1.0), writes=[Vb])
        for c in range(NCH):
            proj_chunk(ud, si, c)

    def attention(l, n, u, ud):
        kind = ud["kind"]
        ticks = []
        for c in range(NCH):
            for m in range(2):
                js = [j for j in range(4 * c - 1, 4 * c + 4) if j >= 0] if kind == "B" else list(range(4 * c + 4))
                for idx, j in enumerate(js):
                    lo = max(j - 4 * c, 0)
                    hi = min(j - 4 * c + 2, 4) if kind == "B" else 4
                    ticks.append((c, m, j, lo, hi, idx == 0, idx == len(js) - 1))
        NT = len(ticks)
        LAT = 2

        def acc_banks(c, m):
            if kind == "A":
                return ACC[2 * m], ACC[2 * m + 1]
            return ACC[(2 * c + m) % 4], None

        def qk(n_):
            c, m, j, lo, hi, first, last = ticks[n_]
            bank = ST[n_ % 3]
            masks = []
            if j >= 4 * c:
                masks.append((j - 4 * c, maskD))
            if kind == "B" and j - 4 * c + 1 < 4:
                masks.append((j - 4 * c + 1, maskP))

            def mm(h):
                ins = h.matmul(ps[bank][:, lo * 128:hi * 128], lhsT=KT[0:KROWS, m, j * 128:(j + 1) * 128],
                               rhs=QT[0:KROWS, m, (4 * c + lo) * 128:(4 * c + hi) * 128], start=True, stop=(not masks))
                for k_, (qq, msk) in enumerate(masks):
                    ins = h.matmul(ps[bank][:, qq * 128:(qq + 1) * 128], lhsT=ident[:], rhs=msk[:], start=False,
                                   stop=(k_ == len(masks) - 1))
                return ins
            sch.op("pe", mm, reads=[KTb[m], QTb[m], KTa[m], QTa[m], B("ident"), B("maskD"), B("maskP")], writes=[psB[bank]])
            sch.op("act", lambda h: h.activation(out=pT[n_ % 4][:, lo * 128:hi * 128], in_=ps[bank][:, lo * 128:hi * 128],
                                                 func=AF.Exp, scale=0.125),
                   reads=[psB[bank]], writes=[B("pT%d" % (n_ % 4))])

        def pv(n_):
            c, m, j, lo, hi, first, last = ticks[n_]
            by, bl = acc_banks(c, m)
            p_ = pT[n_ % 4][:, lo * 128:hi * 128]
            if kind == "A":
                def mm(h):
                    h.matmul(ps[by][:, lo * 128:hi * 128], lhsT=V[:, j, 0:128], rhs=p_, start=first, stop=last)
                    return h.matmul(ps[bl][:, lo * 128:hi * 128], lhsT=onesbf[:], rhs=p_, start=first, stop=last)
                sch.op("pe", mm, reads=[B("pT%d" % (n_ % 4)), Vb, B("onesbf")], writes=[psB[by], psB[bl]])
            else:
                vs = V[:, j, 0:128] if m == 0 else V[:, j, 64:192]

                def mm(h):
                    return h.matmul(ps[by][:, lo * 128:hi * 128], lhsT=vs, rhs=p_, start=first, stop=last,
                                    skip_group_check=(kind == "B"))
                sch.op("pe", mm, reads=[B("pT%d" % (n_ % 4)), Vb], writes=[psB[by]])
            if last:
                finalize(c, m, by, bl)

        def finalize(c, m, by, bl):
            cs = slice(c * 512, (c + 1) * 512)
            outb = [ygTb[u][c]] + AL
            if kind == "A":
                if m == 0:
                    sch.op("act", lambda h: h.activation(out=W[0][:], in_=ps[bl][:, :], func=AF.Ln),
                           reads=[psB[bl]], writes=[Wb[0]])
                    sch.op("act", lambda h: h.activation(out=W[0][:], in_=W[0][:], func=AF.Exp, scale=-1.0),
                           reads=[Wb[0]], writes=[Wb[0]])
                    sch.op("dve", lambda h: h.tensor_tensor(out=W[2][:], in0=ps[by][:, :], in1=W[0][:], op=ALU.mult),
                           reads=[psB[by], Wb[0]], writes=[Wb[2]])
                    return
                sch.op("act", lambda h: h.activation(out=W[1][:], in_=ps[bl][:, :], func=AF.Ln),
                       reads=[psB[bl]], writes=[Wb[1]])
                sch.op("act", lambda h: h.activation(out=W[1][:], in_=W[1][:], func=AF.Exp, scale=-1.0),
                       reads=[Wb[1]], writes=[Wb[1]])
                sch.op("dve", lambda h: h.tensor_tensor(out=W[3][:], in0=ps[by][:, :], in1=W[1][:], op=ALU.mult),
                       reads=[psB[by], Wb[1]], writes=[Wb[3]])
                sch.op("dve", lambda h: h.scalar_tensor_tensor(out=W[4][:], in0=W[3][:], scalar=neglam[:, 0:1], in1=W[2][:],
                                                                op0=ALU.mult, op1=ALU.add),
                       reads=[Wb[3], Wb[2], B("neglam")], writes=[Wb[4]])
                sch.op("dve", lambda h: h.tensor_tensor(out=ysqb[:], in0=W[4][:], in1=W[4][:], op=ALU.mult),
                       reads=[Wb[4]], writes=[B("ysqb")])

                def stage2():
                    sch.op("pe", lambda h: h.matmul(ps[TR][:, :], lhsT=onesbf[:], rhs=ysqb[:], start=True, stop=True),
                           reads=[B("onesbf"), B("ysqb")], writes=[psB[TR]])
                    sch.op("act", lambda h: h.activation(out=W[5][:], in_=ps[TR][:, :], func=AF.Ln, bias=128.0 * SUBLN_EPS),
                           reads=[psB[TR]], writes=[Wb[5]])
                    sch.op("act", lambda h: h.activation(out=W[5][:], in_=W[5][:], func=AF.Exp, scale=-0.5),
                           reads=[Wb[5]], writes=[Wb[5]])
                    sch.op("dve", lambda h: h.tensor_tensor(out=W[4][:], in0=W[4][:], in1=W[5][:], op=ALU.mult),
                           reads=[Wb[4], Wb[5]], writes=[Wb[4]])
                    sch.op("dve", lambda h: h.scalar_tensor_tensor(out=ygT[:, u, cs], in0=W[4][:], scalar=sgcol[:, 0:1],
                                                                    in1=GT[:, cs], op0=ALU.mult, op1=ALU.mult),
                           reads=[Wb[4], GTb[c], B("sgcol")], writes=outb)
                defer(stage2, 8)
                return
            y0, y1 = (0, 64) if m == 0 else (64, 128)
            l0, l1 = (64, 128) if m == 0 else (0, 64)
            rl, tmp = W[m], W[2 + m]
            rlb, tmpb = Wb[m], Wb[2 + m]
            if kind == "B":
                hh = 2 * u + m
                sch.op("act", lambda h: h.activation(out=rl[y0:y1, :], in_=ps[by][l0:l1, :], func=AF.Ln,
                                                     bias=esink[l0:l1, hh:hh + 1]),
                       reads=[psB[by], B("esink")], writes=[rlb])
            elif c < 2:
                sch.op("act", lambda h: h.activation(out=rl[y0:y1, :], in_=ps[by][l0:l1, :], func=AF.Ln),
                       reads=[psB[by]], writes=[rlb])
            if kind == "B" or c < 2:
                sch.op("act", lambda h: h.activation(out=rl[y0:y1, :], in_=rl[y0:y1, :], func=AF.Exp, scale=-1.0),
                       reads=[rlb], writes=[rlb])
            else:
                sch.op("dve", lambda h: h.reciprocal(out=rl[y0:y1, :], in_=ps[by][l0:l1, :]), reads=[psB[by]], writes=[rlb])
            sch.op("dve", lambda h: h.tensor_tensor(out=tmp[y0:y1, :], in0=ps[by][y0:y1, :], in1=rl[y0:y1, :], op=ALU.mult),
                   reads=[psB[by], rlb], writes=[tmpb])
            sch.op("dve", lambda h: h.tensor_tensor(out=ygT[y0:y1, u, cs], in0=tmp[y0:y1, :], in1=GT[y0:y1, cs], op=ALU.mult),
                   reads=[tmpb, GTb[c]], writes=outb)

        for n_ in range(NT + LAT):
            if n_ < NT:
                qk(n_)
            if n_ >= LAT:
                pv(n_ - LAT)
            tick()

    def prefetch_epi(l, n):
        sch.dma("pool", wup[:], wup_d[n][l].rearrange("(ec p) j -> p ec j", p=128), writes=[B("wup")])
        sch.dma("pool", wgm[:], win_d[l, :, GM + n * D:GM + (n + 1) * D].rearrange("(kc p) j -> p kc j", p=128),
                writes=[B("wgm")])

    wo_t = [wslab[i][:].rearrange("p a b -> p (a b)").rearrange("p (k d) -> p k d", k=4) for i in range(2)]
    wo_b = wslab_b[0] + wslab_b[1]

    def prefetch_wo(l):
        for i in range(2):
            sch.dma("pool", wo_t[i], wo_d[l, i * 512:(i + 1) * 512, :].rearrange("(k p) d -> p k d", p=128),
                    writes=wslab_b[i])

    mTs = [(mT, mTb), (mT2, mT2b)]

    def load_mt(c):
        mTc, mTcb = mTs[c % 2]
        sch.dma("sp", mTc, M_d[:, c * 512:(c + 1) * 512].rearrange("(dc p) t -> p dc t", p=128),
                reads=M_b[c], writes=mTcb)

    def epilogue(l, n):
        prv = [W[4][:].bitcast(BF16)[:, 0:512], W[5][:].bitcast(BF16)[:, 0:512]]
        steps = [(c, dc) for c in range(NCH) for dc in range(8)]

        def load_prev(k):
            c, dc = steps[k]
            sch.dma("sp", prv[k % 2], M_d[dc * 128:(dc + 1) * 128, c * 512:(c + 1) * 512],
                    reads=[M_b[c][dc]], writes=[Wb[4 + k % 2]])
        if n > 0:
            load_prev(0)
        for k, (c, dc) in enumerate(steps):
            bu = ACC[(2 * k) % 4]
            bg = ACC[(2 * k + 1) % 4]
            s = k % 2
            if n > 0 and k + 1 < len(steps):
                load_prev(k + 1)
            if n == 2 and k in (12, 20) and NCH >= 4:
                load_mt((k - 12) // 8)

            def mmu(h, c=c, dc=dc, bu=bu):
                ins = None
                for ec in range(4):
                    ins = h.matmul(ps[bu][:], lhsT=wup[:, ec, dc * 128:(dc + 1) * 128],
                                   rhs=ygT[:, ec, c * 512:(c + 1) * 512], start=(ec == 0), stop=(ec == 3))
                return ins
            sch.op("pe", mmu, reads=[B("wup")] + [ygTb[e][c] for e in range(4)] + AL, writes=[psB[bu]])

            def mmg(h, c=c, dc=dc, bg=bg):
                ins = None
                for kc in range(8):
                    ins = h.matmul(ps[bg][:], lhsT=wgm[:, kc, dc * 128:(dc + 1) * 128],
                                   rhs=hT[:, kc, c * 512:(c + 1) * 512], start=(kc == 0), stop=(kc == 7))
                return ins
            sch.op("pe", mmg, reads=[B("wgm"), hTb[c]], writes=[psB[bg]])
            sch.op("act", lambda h, bg=bg, s=s: h.activation(out=sgt[s], in_=ps[bg][:], func=AF.Sigmoid),
                   reads=[psB[bg]], writes=[Wb[s]])
            sch.op("dve", lambda h, bu=bu, s=s: h.tensor_tensor(out=tst[s], in0=ps[bu][:], in1=sgt[s], op=ALU.mult),
                   reads=[psB[bu], Wb[s]], writes=[Wb[2 + s]])
            if n > 0:
                sch.op("dve", lambda h, s=s: h.tensor_tensor(out=tst[s], in0=tst[s], in1=prv[s], op=ALU.add),
                       reads=[Wb[2 + s], Wb[4 + s]], writes=[Wb[2 + s]])
            sch.dma("sp", M_d[dc * 128:(dc + 1) * 128, c * 512:(c + 1) * 512], tst[s],
                    reads=[Wb[2 + s]], writes=[M_b[c][dc]], owner=Wb[2 + s])
            tick()

    final_stores = []

    def final_phase(l):
        last = (l == depth - 1)
        if last:
            sch.dma("sp", gain_bc, fg_d.partition_broadcast(128), writes=[Wb[4], Wb[5]], owner=Wb[4])
        src = x_d if l == 0 else xres_d
        dst = out_d if last else xres_d
        pre = 2 if NCH >= 4 else 0
        if pre == 0:
            load_mt(0)

        def xload(t):
            rd = [xres_b[t]] if l > 0 else []
            sch.dma("sp", xt[t % NXT], src[t * 128:(t + 1) * 128, :], reads=rd, writes=[B("xt%d" % (t % NXT))])
        for c in range(NCH):
            mTc, mTcb = mTs[c % 2]
            if c + 1 < NCH and c + 1 >= pre:
                load_mt(c + 1)
            for tb in range(4):
                t = 4 * c + tb
                s = t % NXT
                xb = B("xt%d" % s)
                la = NXT - 2
                if t == 0:
                    for t_ in range(la):
                        xload(t_)
                if t + la < NB:
                    xload(t + la)
                for dh in range(2):
                    bank = ACC[(2 * t + dh) % 4]

                    def mm(h, tb=tb, dh=dh, bank=bank, mTc=mTc):
                        ins = None
                        for kc in range(8):
                            ins = h.matmul(ps[bank][:], lhsT=mTc[:, kc, tb * 128:(tb + 1) * 128],
                                           rhs=wo_t[kc // 4][:, kc % 4, dh * 512:(dh + 1) * 512],
                                           start=(kc == 0), stop=(kc == 7))
                        return ins
                    sch.op("pe", mm, reads=mTcb + wo_b, writes=[psB[bank]])
                    sch.op("dve", lambda h, s=s, dh=dh, bank=bank: h.tensor_tensor(
                        out=xt[s][:, dh * 512:(dh + 1) * 512], in0=ps[bank][:], in1=xt[s][:, dh * 512:(dh + 1) * 512],
                        op=ALU.add), reads=[psB[bank], xb], writes=[xb])
                if last:
                    st, stb = rms_to_h(xt[s], xb, gain_bc, Wb[4], s, l)
                    sch.op("dve", lambda h, s=s, st=st: h.scalar_tensor_tensor(
                        out=xt[s], in0=xt[s], scalar=st[:, 2:3], in1=gain_bc, op0=ALU.mult, op1=ALU.mult),
                        reads=[xb, stb, Wb[4], Wb[5]], writes=[xb])
                d = sch.dma("sp", dst[t * 128:(t + 1) * 128, :], xt[s], reads=[xb], writes=[xres_b[t]], owner=xb)
                if last:
                    final_stores.append(d)
                tick()

    for l in range(depth):
        layer_scalars(l)
        units = [(n, u, unit_desc(n, u)) for n in range(3) for u in range(4)]
        si_next = load_slab(l, units[0][2])
        load_aug(l, 0, 0, units[0][2])
        pend = []

        def on_tile(t, si=si_next, ud=units[0][2]):
            c, k_ = t // 4, t % 4
            evs = pend[:]
            del pend[:]
            for ev in evs:
                ev()
            if c >= 1:
                proj_steps(ud, si, c - 1, pend)[k_]()
        prologue(l, on_tile=on_tile)
        for ev in pend:
            ev()
        proj_chunk(units[0][2], si_next, NCH - 1)
        forget_gates(l)
        for idx, (n, u, ud) in enumerate(units):
            si = si_next
            if idx + 1 < len(units):
                si_next = load_slab(l, units[idx + 1][2])
            if idx > 0:
                load_aug(l, n, u, ud)
            if u == 2:
                prefetch_epi(l, n)
            if idx > 0:
                unit_projection(l, n, u, ud, si)
            if idx == len(units) - 1:
                prefetch_wo(l)
            attention(l, n, u, ud)
            if u == 3:
                flush()
                epilogue(l, n)
        flush()
        final_phase(l)
    sch.emit(final_wait_ops=final_stores)
    return nc, sch


_CACHE = {}


def kernel(**inputs):
    S, depth = 4096, 4
    x = np.ascontiguousarray(np.asarray(inputs["x"], dtype=np.float32))
    n = x.shape[0]
    if "nc" not in _CACHE:
        _CACHE["nc"] = build(S, depth)[0]
        _CACHE["aug"] = alibi_tables(S)
    nc = _CACHE["nc"]
    augA, augB = _CACHE["aug"]
    shared = {k: np.ascontiguousarray(np.asarray(inputs[k], dtype=np.float32)) for k in (
        "norm_gain", "w_in", "b_forget", "lambda_q1", "lambda_k1", "lambda_q2", "lambda_k2", "subln_gain",
        "sinks", "w_up_a", "w_up_b", "w_up_c", "w_o", "final_gain")}
    shared["augA"] = augA
    shared["augB"] = augB
    in_maps = [dict(shared, x=x[i]) for i in range(n)]
    res = run_bass_kernel_spmd(nc, in_maps, core_ids=list(range(n)))
    return np.stack([r["out"] for r in res.results], axis=0)
```

```python
import math
import numpy as np
import concourse.bass as bass
import concourse.mybir as mybir
from concourse.bass_utils import run_bass_kernel_spmd

F32 = mybir.dt.float32
BF16 = mybir.dt.bfloat16
AF = mybir.ActivationFunctionType
ALU = mybir.AluOpType
AX = mybir.AxisListType

COMPUTE = ("pe", "act", "dve", "pool")


class Buf:
    __slots__ = ("name", "w", "r", "dsem", "dcount", "dlast")

    def __init__(self, name):
        self.name = name
        self.w = None
        self.r = []
        self.dsem = None
        self.dcount = 0
        self.dlast = None


class Op:
    __slots__ = ("eng", "fn", "deps", "sig", "sigidx", "is_dma", "sem", "target")

    def __init__(self, eng, fn, is_dma=False):
        self.eng = eng
        self.fn = fn
        self.deps = []
        self.sig = False
        self.sigidx = None
        self.is_dma = is_dma
        self.sem = None
        self.target = None


class Sched:
    def __init__(self, nc, strict=True):
        self.nc = nc
        self.strict = strict
        self.ops = {e: [] for e in ("pe", "act", "dve", "pool", "sp")}

    def _add(self, op, reads, writes):
        deps = []
        for b in reads:
            if b.w is not None:
                deps.append((b.w, True))
        for b in writes:
            if b.w is not None:
                deps.append((b.w, False))
            for r in b.r:
                deps.append((r, False))
        seen = set()
        for d, raw in deps:
            if d is op or id(d) in seen:
                continue
            if (not d.is_dma) and (not op.is_dma) and d.eng == op.eng:
                if op.eng == "pe" or not (raw or self.strict):
                    continue
            seen.add(id(d))
            op.deps.append(d)
            if not d.is_dma:
                d.sig = True
        for b in reads:
            if not op.is_dma:
                b.r = [x for x in b.r if x.is_dma or x.eng != op.eng]
            b.r.append(op)
        for b in writes:
            b.w = op
            b.r = []
        self.ops[op.eng].append(op)
        return op

    def op(self, eng, fn, reads=(), writes=()):
        return self._add(Op(eng, fn), list(reads), list(writes))

    def dma(self, q, out, in_, reads=(), writes=(), owner=None, **kw):
        reads = list(reads)
        writes = list(writes)
        if owner is None:
            owner = writes[0] if writes else reads[0]
        if owner.dsem is None:
            owner.dsem = self.nc.alloc_semaphore("d_" + owner.name)
        o = Op(q, None, is_dma=True)
        o.sem = owner.dsem
        owner.dcount += 16
        o.target = owner.dcount
        o.fn = lambda h, out=out, in_=in_: h.dma_start(out=out, in_=in_, **kw)
        if owner.dlast is not None:
            o.deps.append(owner.dlast)
        owner.dlast = o
        return self._add(o, reads, writes)

    def emit(self, final_wait_ops=()):
        nc = self.nc
        esem = {e: nc.alloc_semaphore("e_" + e) for e in COMPUTE}
        for e in COMPUTE:
            c = 0
            for o in self.ops[e]:
                if o.sig:
                    c += 1
                    o.sigidx = c
        self.sig_counts = {e: sum(1 for o in self.ops[e] if o.sig) for e in COMPUTE}

        def emit_engine(e, h):
            seen = {}
            for o in self.ops[e]:
                for d in o.deps:
                    if d.is_dma:
                        key, val, sem = ("d", d.sem.num), d.target, d.sem
                    else:
                        key, val, sem = ("e", d.eng), d.sigidx, esem[d.eng]
                    if seen.get(key, 0) >= val:
                        continue
                    seen[key] = val
                    h.wait_ge(sem, val)
                ins = o.fn(h)
                if o.is_dma:
                    ins.then_inc(o.sem, 16)
                elif o.sig:
                    ins.then_inc(esem[e], 1)
            if e == "sp":
                for d in final_wait_ops:
                    h.wait_ge(d.sem, d.target)

        with nc.Block() as block:
            @block.tensor
            def _(h):
                emit_engine("pe", h)

            @block.scalar
            def _(h):
                emit_engine("act", h)

            @block.vector
            def _(h):
                emit_engine("dve", h)

            @block.gpsimd
            def _(h):
                emit_engine("pool", h)

            @block.sync
            def _(h):
                emit_engine("sp", h)


D = 1024
D_IN = 8456
QA, KA, VA, GA = 0, 512, 1024, 1536
QB, KB, VB, GB = 2048, 2560, 2688, 2816
QC, KC, VC, FC, GC = 3328, 3840, 4352, 4864, 4872
GM = 5384
RMS_EPS = 1e-6
SUBLN_EPS = 1e-5
NEG = -30000.0
NAUG = 6
KROWS = 64 + NAUG


def alibi_tables(S):
    pos = np.arange(S)
    blk = (pos // 128).astype(np.float64)
    rem = (pos % 128).astype(np.float64)
    augA = np.zeros((4, 2, NAUG, S), np.float32)
    for h in range(4):
        m = 2.0 ** (-8.0 * (h + 1) / 4)
        augA[h, 0, 0] = -8 * m * 128 * blk
        augA[h, 0, 1] = -8 * m * rem
        augA[h, 0, 2] = 1.0
        augA[h, 0, 3] = 1.0
        augA[h, 1, 0] = 1.0
        augA[h, 1, 1] = 1.0
        augA[h, 1, 2] = 8 * m * 128 * blk
        augA[h, 1, 3] = 8 * m * rem
    augB = np.zeros((8, 2, NAUG, S), np.float32)
    for h in range(8):
        m = 2.0 ** (-8.0 * (h + 1) / 8)
        augB[h, 0, 0] = -8 * m * 128 * blk
        augB[h, 0, 1] = -8 * m * rem
        augB[h, 0, 2] = 8 * m
        augB[h, 0, 3] = 8 * m
        augB[h, 1, 0] = 1.0
        augB[h, 1, 1] = 1.0
        augB[h, 1, 2] = 128 * blk
        augB[h, 1, 3] = rem
    return augA, augB


def unit_desc(n, u):
    if n == 0:
        return dict(kind="A", q=[QA + u * 128, QA + u * 128 + 64], k=[KA + u * 128, KA + u * 128 + 64],
                    v=(VA + u * 128, 128), g=GA + u * 128)
    if n == 1:
        g = u // 2
        return dict(kind="B", q=[QB + (2 * u) * 64, QB + (2 * u + 1) * 64], k=[KB + g * 64, KB + g * 64],
                    v=(VB + g * 64, 64), g=GB + u * 128)
    return dict(kind="C", q=[QC + (2 * u) * 64, QC + (2 * u + 1) * 64], k=[KC + (2 * u) * 64, KC + (2 * u + 1) * 64],
                v=(VC + u * 128, 128), g=GC + u * 128)


def build(S=4096, depth=4):
    NB = S // 128
    NCH = S // 512
    nc = bass.Bass("TRN2", target_bir_lowering=False)

    def din(name, shape):
        return nc.dram_tensor(name, list(shape), F32, kind="ExternalInput").ap()

    x_d = din("x", [S, D])
    ng_d = din("norm_gain", [depth, D])
    win_d = din("w_in", [depth, D, D_IN])
    bf_d = din("b_forget", [depth, 8])
    lq1_d = din("lambda_q1", [depth, 64])
    lk1_d = din("lambda_k1", [depth, 64])
    lq2_d = din("lambda_q2", [depth, 64])
    lk2_d = din("lambda_k2", [depth, 64])
    sg_d = din("subln_gain", [depth, 128])
    sk_d = din("sinks", [depth, 8])
    wup_d = [din("w_up_a", [depth, 512, D]), din("w_up_b", [depth, 512, D]), din("w_up_c", [depth, 512, D])]
    wo_d = din("w_o", [depth, D, D])
    fg_d = din("final_gain", [D])
    augA_d = din("augA", [4, 2, NAUG, S])
    augB_d = din("augB", [8, 2, NAUG, S])
    out_d = nc.dram_tensor("out", [S, D], F32, kind="ExternalOutput").ap()
    xres_d = nc.dram_tensor("xres", [S, D], F32, kind="Internal").ap()
    M_d = nc.dram_tensor("mscr", [D, S], BF16, kind="Internal").ap()
    caug_d = nc.dram_tensor("caug", [2, 8, NAUG, S], BF16, kind="Internal").ap()

    sch = Sched(nc)
    bufs = {}

    def B(name):
        if name not in bufs:
            bufs[name] = Buf(name)
        return bufs[name]

    def sb(name, shape, dt):
        return nc.alloc_sbuf_tensor(name, list(shape), dt)

    hT = sb("hT", [128, 8, S], BF16)
    QT = sb("QT", [128, 2, S], BF16)
    KT = sb("KT", [128, 2, S], BF16)
    V = sb("V", [128, NB, 192], BF16)
    GT = sb("GT", [128, S], BF16)
    ygT = sb("ygT", [128, 4, S], BF16)
    wslab = [sb("wslab%d" % i, [128, 8, 512], BF16) for i in range(2)]
    pT = [sb("pT%d" % i, [128, 512], BF16) for i in range(4)]
    wup = sb("wup", [128, 4, D], BF16)
    wgm = sb("wgm", [128, 8, D], BF16)
    Wall = sb("Wall", [128, 6, 512], F32)
    W = [Wall[:, i, :] for i in range(6)]
    gain_bc = Wall[:, 4:6, :].rearrange("p a b -> p (a b)")
    Wb = None
    sgt = [W[i][:].bitcast(BF16)[:, 0:512] for i in range(2)]
    tst = [W[2 + i][:].bitcast(BF16)[:, 0:512] for i in range(2)]
    ysqb = sb("ysqb", [128, 512], BF16)
    onesbf = sb("onesbf", [128, 128], BF16)
    sgcol = sb("sgcol", [128, 1], F32)
    ident = sb("ident", [128, 128], BF16)
    maskD = sb("maskD", [128, 128], BF16)
    maskP = sb("maskP", [128, 128], BF16)
    lam4 = sb("lam4", [128, 4, 64], F32)
    lamw = sb("lamw", [128, 8], F32)
    neglam = sb("neglam", [128, 1], F32)
    esink = sb("esink", [128, 8], F32)
    negb = sb("negb", [8, 1], F32)
    wfc = sb("wfc", [128, 8, 8], BF16)
    rst = [sb("rst%d" % i, [128, 4], F32) for i in range(4)]
    yraw = ygT[:].rearrange("p a s -> p (a s)")

    def alias(off_bytes, shape, dt):
        n = int(np.prod(shape[1:]))
        esz = 4 if dt == F32 else 2
        a = yraw[:, off_bytes // 2: off_bytes // 2 + n * esz // 2]
        if dt == F32:
            a = a.bitcast(F32)
        if len(shape) == 3:
            a = a.rearrange("p (a b) -> p a b", a=shape[1])
        return a

    NXT = 4 if 8 * S >= 26624 else 2
    xt = [alias(4096 * i, [128, D], F32) for i in range(NXT)]
    hbt = [alias(4096 * NXT + 2048 * i, [128, D], BF16) for i in range(NXT)]
    junk = alias(6144 * NXT, [128, D], BF16)
    qraw = QT[:].rearrange("p a s -> p (a s)")
    kraw = KT[:].rearrange("p a s -> p (a s)")
    assert S >= 2048, "aliasing plan needs S >= 2048"
    big = 2 * S >= 8192
    wgraw = wgm[:].rearrange("p a b -> p (a b)")
    wuraw = wup[:].rearrange("p a b -> p (a b)")

    def tview(raw, off, shape, dt):
        n = int(np.prod(shape[1:]))
        esz = 4 if dt == F32 else 2
        a = raw[0:shape[0], off // 2: off // 2 + n * esz // 2]
        if dt == F32:
            a = a.bitcast(F32)
        if len(shape) == 3:
            a = a.rearrange("p (a b) -> p a b", a=shape[1])
        return a
    fcE = tview(wgraw, 0, [8, 512], F32)
    fcS = tview(wgraw, 2048, [8, 512], F32)
    fcC = [tview(wgraw, 4096, [8, 512], F32), tview(wgraw, 6144, [8, 512], F32)]
    ones8 = tview(wgraw, 8192, [8, 512], F32)
    caq = tview(wgraw, 10240, [8, NAUG, 512], BF16)
    cak = tview(wuraw, 0, [8, NAUG, 512], BF16)
    fcR, fcR2 = fcS, fcE
    mT = kraw[:, 4096:8192].rearrange("p (a b) -> p a b", a=8) if big else sb("mTx", [128, 8, 512], BF16)[:]
    mT2 = (V[:].rearrange("p a b -> p (a b)")[:, 0:4096].rearrange("p (a b) -> p a b", a=8) if big
           else sb("mT2x", [128, 8, 512], BF16)[:])

    ps = [nc.alloc_psum_tensor("ps%d" % i, [128, 512], F32) for i in range(8)]
    psB = [B("ps%d" % i) for i in range(8)]
    ST = (0, 1, 2)
    ACC = (3, 4, 5, 6)
    TR = 7

    hTb = [B("hT%d" % c) for c in range(NCH)]
    QTb = [B("QT0"), B("QT1")]
    KTb = [B("KT0"), B("KT1")]
    QTa = [B("QTa0"), B("QTa1")]
    KTa = [B("KTa0"), B("KTa1")]
    AL = [B("xt%d" % i) for i in range(4)] + [B("hbt%d" % i) for i in range(4)] + [B("junk")]
    mTb = [KTb[1], KTa[1]] if big else [B("mTx")]
    Vb = B("V")
    GTb = [B("GT%d" % c) for c in range(NCH)]
    Wb = [B("W%d" % i) for i in range(6)]
    mT2b = [Vb] if big else [B("mT2x")]
    ygTb = [[B("ygT%d_%d" % (u, c)) for c in range(NCH)] for u in range(4)]
    xres_b = [B("xres%d" % t) for t in range(NB)]
    M_b = [[B("M_%d_%d" % (c, dc)) for dc in range(8)] for c in range(NCH)]
    caug_b = [[B("caug%d_%d" % (sd, c)) for c in range(NCH)] for sd in range(2)]


    sch.op("pool", lambda h: h.memset(ident[:], 1.0), writes=[B("ident")])
    sch.op("pool", lambda h: h.affine_select(out=ident[:], in_=ident[:], pattern=[[-1, 128]],
                                             compare_op=ALU.is_equal, fill=0.0, base=0, channel_multiplier=1),
           reads=[B("ident")], writes=[B("ident")])
    sch.op("pool", lambda h: h.memset(maskD[:], 0.0), writes=[B("maskD")])
    sch.op("pool", lambda h: h.affine_select(out=maskD[:], in_=maskD[:], pattern=[[1, 128]],
                                             compare_op=ALU.is_ge, fill=NEG, base=0, channel_multiplier=-1),
           reads=[B("maskD")], writes=[B("maskD")])
    sch.op("pool", lambda h: h.memset(maskP[:], 0.0), writes=[B("maskP")])
    sch.op("pool", lambda h: h.affine_select(out=maskP[:], in_=maskP[:], pattern=[[-1, 128]],
                                             compare_op=ALU.is_gt, fill=NEG, base=0, channel_multiplier=1),
           reads=[B("maskP")], writes=[B("maskP")])
    sch.op("pool", lambda h: h.memset(onesbf[:], 1.0), writes=[B("onesbf")])

    tick_q = []

    def defer(fn, delay):
        tick_q.append([delay, fn])

    def tick():
        for it in tick_q:
            it[0] -= 1
        ready = [it for it in tick_q if it[0] <= 0]
        tick_q[:] = [it for it in tick_q if it[0] > 0]
        for it in ready:
            it[1]()

    def flush():
        while tick_q:
            tick_q.pop(0)[1]()

    wslab_b = [[B("wslab%d_%d" % (i, p)) for p in range(6)] for i in range(2)]
    slab_ctr = [0]

    def load_slab(l, ud):
        si = slab_ctr[0] % 2
        slab_ctr[0] += 1
        ws = wslab[si]

        def ld(part, dst0, c0, w):
            src = win_d[l, :, c0:c0 + w].rearrange("(kc p) j -> p kc j", p=128)
            sch.dma("pool", ws[:, :, dst0:dst0 + w], src, writes=[wslab_b[si][part]])
        ld(0, 0, ud["q"][0], 64)
        ld(1, 64, ud["q"][1], 64)
        ld(2, 128, ud["k"][0], 64)
        ld(3, 192, ud["k"][1], 64)
        ld(4, 256, ud["v"][0], ud["v"][1])
        ld(5, 384, ud["g"], 128)
        return si

    def load_aug(l, n, u, ud):
        for m in range(2):
            if ud["kind"] == "A":
                sch.dma("pool", QT[64:KROWS, m, :], augA_d[u, 0], writes=[QTa[m]])
                sch.dma("pool", KT[64:KROWS, m, :], augA_d[u, 1], writes=[KTa[m]])
            elif ud["kind"] == "B":
                sch.dma("pool", QT[64:KROWS, m, :], augB_d[2 * u + m, 0], writes=[QTa[m]])
                sch.dma("pool", KT[64:KROWS, m, :], augB_d[2 * u + m, 1], writes=[KTa[m]])
            else:
                sch.dma("sp", QT[64:KROWS, m, :], caug_d[0, 2 * u + m], reads=caug_b[0], writes=[QTa[m]])
                sch.dma("sp", KT[64:KROWS, m, :], caug_d[1, 2 * u + m], reads=caug_b[1], writes=[KTa[m]])

    def rms_to_h(xtile, xb, gbc, gb, slot, l):
        st = rst[slot]
        stb = B("rst%d" % slot)
        sch.op("act", lambda h: h.activation(out=junk, in_=xtile, func=AF.Square, accum_out=st[:, 0:1]),
               reads=[xb], writes=[B("junk"), stb])
        sch.op("act", lambda h: h.activation(out=st[:, 3:4], in_=st[:, 0:1], func=AF.Ln, scale=1.0 / D, bias=RMS_EPS),
               reads=[stb], writes=[stb])
        sch.op("act", lambda h: h.activation(out=st[:, 2:3], in_=st[:, 3:4], func=AF.Exp, scale=-0.5),
               reads=[stb], writes=[stb])
        return st, stb

    def prologue(l, on_tile=None):
        src = x_d if l == 0 else xres_d
        sch.dma("sp", gain_bc, ng_d[l].partition_broadcast(128), writes=[Wb[4], Wb[5]], owner=Wb[4])
        trbanks = (TR, ACC[0])

        def st_a(t):
            s = t % NXT
            rd = [xres_b[t]] if l > 0 else []
            sch.dma("sp", xt[s], src[t * 128:(t + 1) * 128, :], reads=rd, writes=[B("xt%d" % s)])

        def st_b(t):
            s = t % NXT
            st, stb, xb = rst[s], B("rst%d" % s), B("xt%d" % s)
            sch.op("act", lambda h: h.activation(out=junk, in_=xt[s], func=AF.Square, accum_out=st[:, 0:1]),
                   reads=[xb], writes=[B("junk"), stb])

        def st_c(t):
            s = t % NXT
            st, stb, xb, hb = rst[s], B("rst%d" % s), B("xt%d" % s), B("hbt%d" % s)
            sch.op("act", lambda h: h.activation(out=st[:, 3:4], in_=st[:, 0:1], func=AF.Ln, scale=1.0 / D, bias=RMS_EPS),
                   reads=[stb], writes=[stb])
            sch.op("act", lambda h: h.activation(out=st[:, 2:3], in_=st[:, 3:4], func=AF.Exp, scale=-0.5),
                   reads=[stb], writes=[stb])
            sch.op("dve", lambda h: h.scalar_tensor_tensor(
                out=hbt[s], in0=xt[s], scalar=st[:, 2:3], in1=gain_bc, op0=ALU.mult, op1=ALU.mult),
                reads=[xb, stb, Wb[4], Wb[5]], writes=[hb])
            bk = trbanks[t % 2]
            pst = ps[bk][:].bitcast(BF16)

            def tr(h):
                ins = None
                for kc in range(8):
                    ins = h.transpose(pst[:, kc * 128:(kc + 1) * 128], hbt[s][:, kc * 128:(kc + 1) * 128], ident[:])
                return ins
            sch.op("pe", tr, reads=[hb, B("ident")], writes=[psB[bk]])

        def st_d(t):
            bk = trbanks[t % 2]
            pst = ps[bk][:].bitcast(BF16)
            sch.op("dve", lambda h: h.tensor_copy(
                out=hT[:, :, t * 128:(t + 1) * 128], in_=pst.rearrange("p (a b) -> p a b", a=8)),
                reads=[psB[bk]], writes=[hTb[t // 4]])

        ob, oc = (1, 2) if NXT >= 4 else (0, 1)
        od = oc + 1
        for t in range(NB + od):
            if t < NB:
                st_a(t)
            if 0 <= t - ob < NB:
                st_b(t - ob)
            if 0 <= t - oc < NB:
                st_c(t - oc)
            if 0 <= t - od < NB:
                st_d(t - od)
                if on_tile is not None:
                    on_tile(t - od)

    def layer_scalars(l):
        lb = B("lam")
        for i, d_ in enumerate((lq1_d, lk1_d, lq2_d, lk2_d)):
            sch.dma("sp", lam4[:, i, :], d_[l].partition_broadcast(128), writes=[B("lam4_%d" % i)], owner=B("lam4_%d" % i))
        rl = [B("lam4_%d" % i) for i in range(4)]
        sch.op("dve", lambda h: h.tensor_tensor(out=lam4[:, 0, :], in0=lam4[:, 0, :], in1=lam4[:, 1, :], op=ALU.mult),
               reads=rl[:2], writes=[rl[0]])
        sch.op("dve", lambda h: h.tensor_tensor(out=lam4[:, 2, :], in0=lam4[:, 2, :], in1=lam4[:, 3, :], op=ALU.mult),
               reads=rl[2:], writes=[rl[2]])
        sch.op("dve", lambda h: h.reduce_sum(out=lamw[:, 0:1], in_=lam4[:, 0, :], axis=AX.X), reads=[rl[0]], writes=[lb])
        sch.op("dve", lambda h: h.reduce_sum(out=lamw[:, 1:2], in_=lam4[:, 2, :], axis=AX.X), reads=[rl[2], lb], writes=[lb])
        sch.op("act", lambda h: h.activation(out=lamw[:, 2:4], in_=lamw[:, 0:2], func=AF.Exp), reads=[lb], writes=[lb])
        lam_init = 0.8 - 0.6 * math.exp(-0.3 * l)
        sch.op("dve", lambda h: h.tensor_tensor(out=lamw[:, 4:5], in0=lamw[:, 3:4], in1=lamw[:, 2:3], op=ALU.subtract),
               reads=[lb], writes=[lb])
        sch.op("dve", lambda h: h.tensor_scalar(out=neglam[:], in0=lamw[:, 4:5], scalar1=-lam_init, scalar2=None,
                                                op0=ALU.add), reads=[lb], writes=[B("neglam")])
        sch.dma("sp", sgcol[:], sg_d[l].rearrange("(a b) -> a b", b=1), writes=[B("sgcol")])
        sch.op("dve", lambda h: h.tensor_scalar(out=sgcol[:], in0=sgcol[:], scalar1=(1.0 - lam_init) * math.sqrt(128.0),
                                                scalar2=None, op0=ALU.mult), reads=[B("sgcol")], writes=[B("sgcol")])
        sch.dma("sp", esink[:], sk_d[l].partition_broadcast(128), writes=[B("esink")])
        sch.op("act", lambda h: h.activation(out=esink[:], in_=esink[:], func=AF.Exp), reads=[B("esink")], writes=[B("esink")])
        sch.dma("sp", negb[:], bf_d[l].rearrange("(a b) -> a b", b=1), writes=[B("negb")])
        sch.op("dve", lambda h: h.tensor_scalar(out=negb[:], in0=negb[:], scalar1=-1.0, scalar2=None, op0=ALU.mult),
               reads=[B("negb")], writes=[B("negb")])

    def forget_gates(l):
        wb = B("wfc")
        sch.dma("pool", wfc[:], win_d[l, :, FC:FC + 8].rearrange("(kc p) j -> p kc j", p=128), writes=[wb])
        FGT = [B(nm) for nm in ("fcE", "fcS", "fcC0", "fcC1", "ones8", "caq", "cak")]
        FG = [B("wgm"), B("wup")]
        state = {"prev": None}
        qb_, kb_ = B("caq"), B("cak")

        def entry():
            sch.op("pool", lambda h: h.memset(ones8, 1.0), writes=FG + FGT)
            sch.op("pool", lambda h: h.memset(caq, 1.0), reads=[B("ones8")], writes=[qb_])
            sch.op("pool", lambda h: h.memset(cak, 1.0), reads=[B("ones8")], writes=[kb_])

        def stage1(c):
            def mm(h):
                ins = None
                for kc in range(8):
                    ins = h.matmul(ps[TR][0:8, :], lhsT=wfc[:, kc, :], rhs=hT[:, kc, c * 512:(c + 1) * 512],
                                   start=(kc == 0), stop=(kc == 7))
                return ins
            sch.op("pe", mm, reads=[wb, hTb[c]], writes=[psB[TR]])
            sch.op("act", lambda h: h.activation(out=fcE[:], in_=ps[TR][0:8, :], func=AF.Exp,
                                                 bias=negb[:, 0:1], scale=-1.0),
                   reads=[psB[TR], B("negb")], writes=[B("fcE")])
            sch.op("act", lambda h: h.activation(out=fcS[:], in_=fcE[:], func=AF.Ln, bias=1.0, scale=1.0),
                   reads=[B("fcE")], writes=[B("fcS")])

        def stage2(c):
            sch.op("dve", lambda h: h.tensor_scalar(out=fcS[:], in0=fcS[:], scalar1=-8.0, scalar2=None, op0=ALU.mult),
                   reads=[B("fcS")], writes=[B("fcS")])
            cur = fcC[c % 2]
            curb = B("fcC%d" % (c % 2))
            prev = state["prev"]
            init = 0.0 if prev is None else prev[0][:, 511:512]
            rd = [B("ones8"), B("fcS")] + ([prev[1]] if prev is not None else [])
            sch.op("dve", lambda h: h.tensor_tensor_scan(
                out=cur[:], data0=ones8[:], data1=fcS[:], initial=init, op0=ALU.mult, op1=ALU.add),
                reads=rd, writes=[curb])
            state["prev"] = (cur, curb)
            sch.op("dve", lambda h: h.tensor_copy(out=caq[:, 0, :], in_=cur[:]), reads=[curb], writes=[qb_])
            sch.op("dve", lambda h: h.tensor_tensor(out=fcR[:], in0=cur[:], in1=caq[:, 0, :], op=ALU.subtract),
                   reads=[curb, qb_], writes=[B("fcS")])
            sch.op("dve", lambda h: h.tensor_copy(out=caq[:, 1, :], in_=fcR[:]), reads=[B("fcS")], writes=[qb_])
            sch.op("dve", lambda h: h.tensor_tensor(out=fcR2[:], in0=fcR[:], in1=caq[:, 1, :], op=ALU.subtract),
                   reads=[B("fcS"), qb_], writes=[B("fcE")])
            sch.op("dve", lambda h: h.tensor_copy(out=caq[:, 2, :], in_=fcR2[:]), reads=[B("fcE")], writes=[qb_])
            sch.op("dve", lambda h: h.tensor_scalar(out=cak[:, 3:6, :], in0=caq[:, 0:3, :], scalar1=-1.0, scalar2=None,
                                                    op0=ALU.mult), reads=[qb_], writes=[kb_])
            sch.dma("sp", caug_d[0, :, :, c * 512:(c + 1) * 512], caq[:], reads=[qb_], writes=[caug_b[0][c]], owner=qb_)
            sch.dma("sp", caug_d[1, :, :, c * 512:(c + 1) * 512], cak[:], reads=[kb_], writes=[caug_b[1][c]], owner=kb_)

        def exit_guard():
            sch.op("pool", lambda h: h.memset(ones8[:, 0:1], 1.0), writes=FG + FGT)

        defer(entry, 1)
        for c in range(NCH):
            defer(lambda c=c: stage1(c), 2 + 2 * c)
            defer(lambda c=c: stage2(c), 3 + 2 * c)
        defer(exit_guard, 4 + 2 * NCH)

    proj_ctr = [0]

    def proj_chunk(ud, si, c):
        for st_ in proj_steps(ud, si, c):
            st_()

    def proj_steps(ud, si, c, later=None):
        return [lambda: proj_qk(ud, si, c, 0, later), lambda: proj_qk(ud, si, c, 1, later),
                lambda: proj_v(ud, si, c, later), lambda: proj_g(ud, si, c, later)]

    def _evac(fn, later):
        if later is None:
            fn()
        else:
            later.append(fn)

    def proj_qk(ud, si, c, which, later=None):
        ws = wslab[si]
        wsb = wslab_b[si]
        kind = ud["kind"]
        for (dstT, dstb, c0, eng) in (((QT, QTb, 0, "dve"), (KT, KTb, 128, "act"))[which],):
            bank = ST[proj_ctr[0] % 3]
            proj_ctr[0] += 1

            def mm(h, bank=bank, c0=c0):
                ins = None
                for kc in range(8):
                    ins = h.matmul(ps[bank][:, :], lhsT=ws[:, kc, c0:c0 + 128],
                                   rhs=hT[:, kc, c * 512:(c + 1) * 512], start=(kc == 0), stop=(kc == 7))
                return ins
            sch.op("pe", mm, reads=wsb + [hTb[c]], writes=[psB[bank]])

            def ev(bank=bank, dstT=dstT, dstb=dstb, eng=eng):
                for m in range(2):
                    if eng == "dve":
                        sch.op("dve", lambda h, m=m: h.tensor_copy(
                            out=dstT[0:64, m, c * 512:(c + 1) * 512], in_=ps[bank][m * 64:(m + 1) * 64, :]),
                            reads=[psB[bank]], writes=[dstb[m]])
                    else:
                        sch.op("act", lambda h, m=m: h.activation(
                            out=dstT[0:64, m, c * 512:(c + 1) * 512], in_=ps[bank][m * 64:(m + 1) * 64, :], func=AF.Copy),
                            reads=[psB[bank]], writes=[dstb[m]])
            _evac(ev, later)
            tick()

    def proj_v(ud, si, c, later=None):
        ws = wslab[si]
        wsb = wslab_b[si]
        kind = ud["kind"]
        vw = ud["v"][1]
        t4 = c
        bank = ST[proj_ctr[0] % 3]
        proj_ctr[0] += 1

        def mmv(h):
            ins = None
            for tt in range(4):
                t = 4 * t4 + tt
                for kc in range(8):
                    ins = h.matmul(ps[bank][:, tt * 128:tt * 128 + vw], lhsT=hT[:, kc, t * 128:(t + 1) * 128],
                                   rhs=ws[:, kc, 256:256 + vw], start=(kc == 0), stop=(kc == 7))
            return ins
        sch.op("pe", mmv, reads=wsb + [hTb[t4]], writes=[psB[bank]])
        pv4 = ps[bank][:].rearrange("p (a b) -> p a b", a=4)
        blk = slice(4 * t4, 4 * t4 + 4)
        if kind == "A":
            moves = [((0, 128), (0, 128))]
        elif kind == "B":
            moves = [((0, 64), (0, 64)), ((128, 192), (0, 64))]
        else:
            moves = [((0, 64), (0, 64)), ((128, 192), (64, 128))]
        def ev():
            for (d0, d1), (s0, s1) in moves:
                sch.op("act", lambda h, d0=d0, d1=d1, s0=s0, s1=s1: h.activation(
                    out=V[:, blk, d0:d1], in_=pv4[:, :, s0:s1], func=AF.Copy), reads=[psB[bank]], writes=[Vb])
        _evac(ev, later)
        tick()

    def proj_g(ud, si, c, later=None):
        ws = wslab[si]
        wsb = wslab_b[si]
        bank2 = ST[proj_ctr[0] % 3]
        proj_ctr[0] += 1

        def mmg(h):
            ins = None
            for kc in range(8):
                ins = h.matmul(ps[bank2][:, :], lhsT=ws[:, kc, 384:512], rhs=hT[:, kc, c * 512:(c + 1) * 512],
                               start=(kc == 0), stop=(kc == 7))
            return ins
        sch.op("pe", mmg, reads=wsb + [hTb[c]], writes=[psB[bank2]])
        def ev():
            sch.op("act", lambda h: h.activation(out=GT[:, c * 512:(c + 1) * 512], in_=ps[bank2][:, :],
                                                 func=AF.Silu), reads=[psB[bank2]], writes=[GTb[c]])
        _evac(ev, later)
        tick()

    def unit_projection(l, n, u, ud, si):
        if ud["kind"] == "B" and u == 0:
            sch.op("pool", lambda h: h.memset(V[:, :, 64:128], 1.0), writes=[Vb])
        for c in range(NCH):
            proj_chunk(ud, si, c)

    def attention(l, n, u, ud):
        kind = ud["kind"]
        ticks = []
        for c in range(NCH):
            for m in range(2):
                js = [j for j in range(4 * c - 1, 4 * c + 4) if j >= 0] if kind == "B" else list(range(4 * c + 4))
                for idx, j in enumerate(js):
                    lo = max(j - 4 * c, 0)
                    hi = min(j - 4 * c + 2, 4) if kind == "B" else 4
                    ticks.append((c, m, j, lo, hi, idx == 0, idx == len(js) - 1))
        NT = len(ticks)
        LAT = 2

        def acc_banks(c, m):
            if kind == "A":
                return ACC[2 * m], ACC[2 * m + 1]
            return ACC[(2 * c + m) % 4], None

        def qk(n_):
            c, m, j, lo, hi, first, last = ticks[n_]
            bank = ST[n_ % 3]
            masks = []
            if j >= 4 * c:
                masks.append((j - 4 * c, maskD))
            if kind == "B" and j - 4 * c + 1 < 4:
                masks.append((j - 4 * c + 1, maskP))

            def mm(h):
                ins = h.matmul(ps[bank][:, lo * 128:hi * 128], lhsT=KT[0:KROWS, m, j * 128:(j + 1) * 128],
                               rhs=QT[0:KROWS, m, (4 * c + lo) * 128:(4 * c + hi) * 128], start=True, stop=(not masks))
                for k_, (qq, msk) in enumerate(masks):
                    ins = h.matmul(ps[bank][:, qq * 128:(qq + 1) * 128], lhsT=ident[:], rhs=msk[:], start=False,
                                   stop=(k_ == len(masks) - 1))
                return ins
            sch.op("pe", mm, reads=[KTb[m], QTb[m], KTa[m], QTa[m], B("ident"), B("maskD"), B("maskP")], writes=[psB[bank]])
            sch.op("act", lambda h: h.activation(out=pT[n_ % 4][:, lo * 128:hi * 128], in_=ps[bank][:, lo * 128:hi * 128],
                                                 func=AF.Exp, scale=0.125),
                   reads=[psB[bank]], writes=[B("pT%d" % (n_ % 4))])

        def pv(n_):
            c, m, j, lo, hi, first, last = ticks[n_]
            by, bl = acc_banks(c, m)
            p_ = pT[n_ % 4][:, lo * 128:hi * 128]
            if kind == "A":
                def mm(h):
                    h.matmul(ps[by][:, lo * 128:hi * 128], lhsT=V[:, j, 0:128], rhs=p_, start=first, stop=last)
                    return h.matmul(ps[bl][:, lo * 128:hi * 128], lhsT=onesbf[:], rhs=p_, start=first, stop=last)
                sch.op("pe", mm, reads=[B("pT%d" % (n_ % 4)), Vb, B("onesbf")], writes=[psB[by], psB[bl]])
            else:
                vs = V[:, j, 0:128] if m == 0 else V[:, j, 64:192]

                def mm(h):
                    return h.matmul(ps[by][:, lo * 128:hi * 128], lhsT=vs, rhs=p_, start=first, stop=last,
                                    skip_group_check=(kind == "B"))
                sch.op("pe", mm, reads=[B("pT%d" % (n_ % 4)), Vb], writes=[psB[by]])
            if last:
                finalize(c, m, by, bl)

        def finalize(c, m, by, bl):
            cs = slice(c * 512, (c + 1) * 512)
            outb = [ygTb[u][c]] + AL
            def recip_l(wi, bl_):
                if c < 2:
                    sch.op("act", lambda h: h.activation(out=W[wi][:], in_=ps[bl_][:, :], func=AF.Ln),
                           reads=[psB[bl_]], writes=[Wb[wi]])
                    sch.op("act", lambda h: h.activation(out=W[wi][:], in_=W[wi][:], func=AF.Exp, scale=-1.0),
                           reads=[Wb[wi]], writes=[Wb[wi]])
                else:
                    sch.op("dve", lambda h: h.reciprocal(out=W[wi][:], in_=ps[bl_][:, :]), reads=[psB[bl_]], writes=[Wb[wi]])
            if kind == "A":
                if m == 0:
                    recip_l(0, bl)
                    sch.op("dve", lambda h: h.tensor_tensor(out=W[2][:], in0=ps[by][:, :], in1=W[0][:], op=ALU.mult),
                           reads=[psB[by], Wb[0]], writes=[Wb[2]])
                    return
                recip_l(1, bl)
                sch.op("dve", lambda h: h.tensor_tensor(out=W[3][:], in0=ps[by][:, :], in1=W[1][:], op=ALU.mult),
                       reads=[psB[by], Wb[1]], writes=[Wb[3]])
                sch.op("dve", lambda h: h.scalar_tensor_tensor(out=W[4][:], in0=W[3][:], scalar=neglam[:, 0:1], in1=W[2][:],
                                                                op0=ALU.mult, op1=ALU.add),
                       reads=[Wb[3], Wb[2], B("neglam")], writes=[Wb[4]])
                sch.op("dve", lambda h: h.tensor_tensor(out=ysqb[:], in0=W[4][:], in1=W[4][:], op=ALU.mult),
                       reads=[Wb[4]], writes=[B("ysqb")])

                def stage2():
                    sch.op("pe", lambda h: h.matmul(ps[TR][:, :], lhsT=onesbf[:], rhs=ysqb[:], start=True, stop=True),
                           reads=[B("onesbf"), B("ysqb")], writes=[psB[TR]])
                    sch.op("act", lambda h: h.activation(out=W[5][:], in_=ps[TR][:, :], func=AF.Ln, bias=128.0 * SUBLN_EPS),
                           reads=[psB[TR]], writes=[Wb[5]])
                    sch.op("act", lambda h: h.activation(out=W[5][:], in_=W[5][:], func=AF.Exp, scale=-0.5),
                           reads=[Wb[5]], writes=[Wb[5]])
                    sch.op("dve", lambda h: h.tensor_tensor(out=W[4][:], in0=W[4][:], in1=W[5][:], op=ALU.mult),
                           reads=[Wb[4], Wb[5]], writes=[Wb[4]])
                    sch.op("dve", lambda h: h.scalar_tensor_tensor(out=ygT[:, u, cs], in0=W[4][:], scalar=sgcol[:, 0:1],
                                                                    in1=GT[:, cs], op0=ALU.mult, op1=ALU.mult),
                           reads=[Wb[4], GTb[c], B("sgcol")], writes=outb)
                defer(stage2, 8)
                return
            y0, y1 = (0, 64) if m == 0 else (64, 128)
            l0, l1 = (64, 128) if m == 0 else (0, 64)
            rl, tmp = W[m], W[2 + m]
            rlb, tmpb = Wb[m], Wb[2 + m]
            if kind == "B":
                hh = 2 * u + m
                sch.op("act", lambda h: h.activation(out=rl[y0:y1, :], in_=ps[by][l0:l1, :], func=AF.Ln,
                                                     bias=esink[l0:l1, hh:hh + 1]),
                       reads=[psB[by], B("esink")], writes=[rlb])
            elif c < 2:
                sch.op("act", lambda h: h.activation(out=rl[y0:y1, :], in_=ps[by][l0:l1, :], func=AF.Ln),
                       reads=[psB[by]], writes=[rlb])
            if kind == "B" or c < 2:
                sch.op("act", lambda h: h.activation(out=rl[y0:y1, :], in_=rl[y0:y1, :], func=AF.Exp, scale=-1.0),
                       reads=[rlb], writes=[rlb])
            else:
                sch.op("dve", lambda h: h.reciprocal(out=rl[y0:y1, :], in_=ps[by][l0:l1, :]), reads=[psB[by]], writes=[rlb])
            sch.op("dve", lambda h: h.tensor_tensor(out=tmp[y0:y1, :], in0=ps[by][y0:y1, :], in1=rl[y0:y1, :], op=ALU.mult),
                   reads=[psB[by], rlb], writes=[tmpb])
            sch.op("dve", lambda h: h.tensor_tensor(out=ygT[y0:y1, u, cs], in0=tmp[y0:y1, :], in1=GT[y0:y1, cs], op=ALU.mult),
                   reads=[tmpb, GTb[c]], writes=outb)

        for n_ in range(NT + LAT):
            if n_ < NT:
                qk(n_)
            if n_ >= LAT:
                pv(n_ - LAT)
            tick()

    def prefetch_epi(l, n):
        sch.dma("pool", wup[:], wup_d[n][l].rearrange("(ec p) j -> p ec j", p=128), writes=[B("wup")])
        sch.dma("pool", wgm[:], win_d[l, :, GM + n * D:GM + (n + 1) * D].rearrange("(kc p) j -> p kc j", p=128),
                writes=[B("wgm")])

    wo_t = [wslab[i][:].rearrange("p a b -> p (a b)").rearrange("p (k d) -> p k d", k=4) for i in range(2)]
    wo_b = wslab_b[0] + wslab_b[1]

    def prefetch_wo(l):
        for i in range(2):
            sch.dma("pool", wo_t[i], wo_d[l, i * 512:(i + 1) * 512, :].rearrange("(k p) d -> p k d", p=128),
                    writes=wslab_b[i])

    mTs = [(mT, mTb), (mT2, mT2b)]

    def load_mt(c):
        mTc, mTcb = mTs[c % 2]
        sch.dma("sp", mTc, M_d[:, c * 512:(c + 1) * 512].rearrange("(dc p) t -> p dc t", p=128),
                reads=M_b[c], writes=mTcb)

    def epilogue(l, n):
        prv = [W[4][:].bitcast(BF16)[:, 0:512], W[5][:].bitcast(BF16)[:, 0:512]]
        steps = [(c, dc) for c in range(NCH) for dc in range(8)]

        def load_prev(k):
            c, dc = steps[k]
            sch.dma("sp", prv[k % 2], M_d[dc * 128:(dc + 1) * 128, c * 512:(c + 1) * 512],
                    reads=[M_b[c][dc]], writes=[Wb[4 + k % 2]])
        if n > 0:
            load_prev(0)
        for k, (c, dc) in enumerate(steps):
            bu = ACC[(2 * k) % 4]
            bg = ACC[(2 * k + 1) % 4]
            s = k % 2
            if n > 0 and k + 1 < len(steps):
                load_prev(k + 1)
            if n == 2 and k in (12, 20) and NCH >= 4:
                load_mt((k - 12) // 8)

            def mmu(h, c=c, dc=dc, bu=bu):
                ins = None
                for ec in range(4):
                    ins = h.matmul(ps[bu][:], lhsT=wup[:, ec, dc * 128:(dc + 1) * 128],
                                   rhs=ygT[:, ec, c * 512:(c + 1) * 512], start=(ec == 0), stop=(ec == 3))
                return ins
            sch.op("pe", mmu, reads=[B("wup")] + [ygTb[e][c] for e in range(4)] + AL, writes=[psB[bu]])

            def mmg(h, c=c, dc=dc, bg=bg):
                ins = None
                for kc in range(8):
                    ins = h.matmul(ps[bg][:], lhsT=wgm[:, kc, dc * 128:(dc + 1) * 128],
                                   rhs=hT[:, kc, c * 512:(c + 1) * 512], start=(kc == 0), stop=(kc == 7))
                return ins
            sch.op("pe", mmg, reads=[B("wgm"), hTb[c]], writes=[psB[bg]])
            sch.op("act", lambda h, bg=bg, s=s: h.activation(out=sgt[s], in_=ps[bg][:], func=AF.Sigmoid),
                   reads=[psB[bg]], writes=[Wb[s]])
            sch.op("dve", lambda h, bu=bu, s=s: h.tensor_tensor(out=tst[s], in0=ps[bu][:], in1=sgt[s], op=ALU.mult),
                   reads=[psB[bu], Wb[s]], writes=[Wb[2 + s]])
            if n > 0:
                sch.op("dve", lambda h, s=s: h.tensor_tensor(out=tst[s], in0=tst[s], in1=prv[s], op=ALU.add),
                       reads=[Wb[2 + s], Wb[4 + s]], writes=[Wb[2 + s]])
            sch.dma("sp", M_d[dc * 128:(dc + 1) * 128, c * 512:(c + 1) * 512], tst[s],
                    reads=[Wb[2 + s]], writes=[M_b[c][dc]], owner=Wb[2 + s])
            tick()

    final_stores = []

    def final_phase(l):
        last = (l == depth - 1)
        if last:
            sch.dma("sp", gain_bc, fg_d.partition_broadcast(128), writes=[Wb[4], Wb[5]], owner=Wb[4])
        src = x_d if l == 0 else xres_d
        dst = out_d if last else xres_d
        pre = 2 if NCH >= 4 else 0
        if pre == 0:
            load_mt(0)

        def xload(t):
            rd = [xres_b[t]] if l > 0 else []
            sch.dma("sp", xt[t % NXT], src[t * 128:(t + 1) * 128, :], reads=rd, writes=[B("xt%d" % (t % NXT))])
        for c in range(NCH):
            mTc, mTcb = mTs[c % 2]
            if c + 1 < NCH and c + 1 >= pre:
                load_mt(c + 1)
            for tb in range(4):
                t = 4 * c + tb
                s = t % NXT
                xb = B("xt%d" % s)
                la = NXT - 2
                if t == 0:
                    for t_ in range(la):
                        xload(t_)
                if t + la < NB:
                    xload(t + la)
                for dh in range(2):
                    bank = ACC[(2 * t + dh) % 4]

                    def mm(h, tb=tb, dh=dh, bank=bank, mTc=mTc):
                        ins = None
                        for kc in range(8):
                            ins = h.matmul(ps[bank][:], lhsT=mTc[:, kc, tb * 128:(tb + 1) * 128],
                                           rhs=wo_t[kc // 4][:, kc % 4, dh * 512:(dh + 1) * 512],
                                           start=(kc == 0), stop=(kc == 7))
                        return ins
                    sch.op("pe", mm, reads=mTcb + wo_b, writes=[psB[bank]])
                    sch.op("dve", lambda h, s=s, dh=dh, bank=bank: h.tensor_tensor(
                        out=xt[s][:, dh * 512:(dh + 1) * 512], in0=ps[bank][:], in1=xt[s][:, dh * 512:(dh + 1) * 512],
                        op=ALU.add), reads=[psB[bank], xb], writes=[xb])
                if last:
                    st, stb = rms_to_h(xt[s], xb, gain_bc, Wb[4], s, l)
                    sch.op("dve", lambda h, s=s, st=st: h.scalar_tensor_tensor(
                        out=xt[s], in0=xt[s], scalar=st[:, 2:3], in1=gain_bc, op0=ALU.mult, op1=ALU.mult),
                        reads=[xb, stb, Wb[4], Wb[5]], writes=[xb])
                d = sch.dma("sp", dst[t * 128:(t + 1) * 128, :], xt[s], reads=[xb], writes=[xres_b[t]], owner=xb)
                if last:
                    final_stores.append(d)
                tick()

    for l in range(depth):
        layer_scalars(l)
        units = [(n, u, unit_desc(n, u)) for n in range(3) for u in range(4)]
        si_next = load_slab(l, units[0][2])
        load_aug(l, 0, 0, units[0][2])
        pend = []

        def on_tile(t, si=si_next, ud=units[0][2]):
            c, k_ = t // 4, t % 4
            evs = pend[:]
            del pend[:]
            for ev in evs:
                ev()
            if c >= 1:
                proj_steps(ud, si, c - 1, pend)[k_]()
        prologue(l, on_tile=on_tile)
        for ev in pend:
            ev()
        proj_chunk(units[0][2], si_next, NCH - 1)
        forget_gates(l)
        for idx, (n, u, ud) in enumerate(units):
            si = si_next
            if idx + 1 < len(units):
                si_next = load_slab(l, units[idx + 1][2])
            if idx > 0:
                load_aug(l, n, u, ud)
            if u == 2:
                prefetch_epi(l, n)
            if idx > 0:
                unit_projection(l, n, u, ud, si)
            if idx == len(units) - 1:
                prefetch_wo(l)
            attention(l, n, u, ud)
            if u == 3:
                flush()
                epilogue(l, n)
        flush()
        final_phase(l)
    sch.emit(final_wait_ops=final_stores)
    return nc, sch


_CACHE = {}


def kernel(**inputs):
    S, depth = 4096, 4
    x = np.ascontiguousarray(np.asarray(inputs["x"], dtype=np.float32))
    n = x.shape[0]
    if "nc" not in _CACHE:
        _CACHE["nc"] = build(S, depth)[0]
        _CACHE["aug"] = alibi_tables(S)
    nc = _CACHE["nc"]
    augA, augB = _CACHE["aug"]
    shared = {k: np.ascontiguousarray(np.asarray(inputs[k], dtype=np.float32)) for k in (
        "norm_gain", "w_in", "b_forget", "lambda_q1", "lambda_k1", "lambda_q2", "lambda_k2", "subln_gain",
        "sinks", "w_up_a", "w_up_b", "w_up_c", "w_o", "final_gain")}
    shared["augA"] = augA
    shared["augB"] = augB
    in_maps = [dict(shared, x=x[i]) for i in range(n)]
    res = run_bass_kernel_spmd(nc, in_maps, core_ids=list(range(n)))
    return np.stack([r["out"] for r in res.results], axis=0)
```

```python
import math
import numpy as np
import concourse.bass as bass
import concourse.mybir as mybir
from concourse.bass_utils import run_bass_kernel_spmd

F32 = mybir.dt.float32
BF16 = mybir.dt.bfloat16
AF = mybir.ActivationFunctionType
ALU = mybir.AluOpType
AX = mybir.AxisListType

COMPUTE = ("pe", "act", "dve", "pool")


class Buf:
    __slots__ = ("name", "w", "r", "dsem", "dcount", "dlast")

    def __init__(self, name):
        self.name = name
        self.w = None
        self.r = []
        self.dsem = None
        self.dcount = 0
        self.dlast = None


class Op:
    __slots__ = ("eng", "fn", "deps", "sig", "sigidx", "is_dma", "sem", "target")

    def __init__(self, eng, fn, is_dma=False):
        self.eng = eng
        self.fn = fn
        self.deps = []
        self.sig = False
        self.sigidx = None
        self.is_dma = is_dma
        self.sem = None
        self.target = None


class Sched:
    def __init__(self, nc, strict=True):
        self.nc = nc
        self.strict = strict
        self.ops = {e: [] for e in ("pe", "act", "dve", "pool", "sp")}

    def _add(self, op, reads, writes):
        deps = []
        for b in reads:
            if b.w is not None:
                deps.append((b.w, True))
        for b in writes:
            if b.w is not None:
                deps.append((b.w, False))
            for r in b.r:
                deps.append((r, False))
        seen = set()
        for d, raw in deps:
            if d is op or id(d) in seen:
                continue
            if (not d.is_dma) and (not op.is_dma) and d.eng == op.eng:
                if op.eng == "pe" or not (raw or self.strict):
                    continue
            seen.add(id(d))
            op.deps.append(d)
            if not d.is_dma:
                d.sig = True
        for b in reads:
            if not op.is_dma:
                b.r = [x for x in b.r if x.is_dma or x.eng != op.eng]
            b.r.append(op)
        for b in writes:
            b.w = op
            b.r = []
        self.ops[op.eng].append(op)
        return op

    def op(self, eng, fn, reads=(), writes=()):
        return self._add(Op(eng, fn), list(reads), list(writes))

    def dma(self, q, out, in_, reads=(), writes=(), owner=None, **kw):
        reads = list(reads)
        writes = list(writes)
        if owner is None:
            owner = writes[0] if writes else reads[0]
        if owner.dsem is None:
            owner.dsem = self.nc.alloc_semaphore("d_" + owner.name)
        o = Op(q, None, is_dma=True)
        o.sem = owner.dsem
        owner.dcount += 16
        o.target = owner.dcount
        o.fn = lambda h, out=out, in_=in_: h.dma_start(out=out, in_=in_, **kw)
        if owner.dlast is not None:
            o.deps.append(owner.dlast)
        owner.dlast = o
        return self._add(o, reads, writes)

    def emit(self, final_wait_ops=()):
        nc = self.nc
        esem = {e: nc.alloc_semaphore("e_" + e) for e in COMPUTE}
        for e in COMPUTE:
            c = 0
            for o in self.ops[e]:
                if o.sig:
                    c += 1
                    o.sigidx = c
        self.sig_counts = {e: sum(1 for o in self.ops[e] if o.sig) for e in COMPUTE}

        def emit_engine(e, h):
            seen = {}
            for o in self.ops[e]:
                for d in o.deps:
                    if d.is_dma:
                        key, val, sem = ("d", d.sem.num), d.target, d.sem
                    else:
                        key, val, sem = ("e", d.eng), d.sigidx, esem[d.eng]
                    if seen.get(key, 0) >= val:
                        continue
                    seen[key] = val
                    h.wait_ge(sem, val)
                ins = o.fn(h)
                if o.is_dma:
                    ins.then_inc(o.sem, 16)
                elif o.sig:
                    ins.then_inc(esem[e], 1)
            if e == "sp":
                for d in final_wait_ops:
                    h.wait_ge(d.sem, d.target)

        with nc.Block() as block:
            @block.tensor
            def _(h):
                emit_engine("pe", h)

            @block.scalar
            def _(h):
                emit_engine("act", h)

            @block.vector
            def _(h):
                emit_engine("dve", h)

            @block.gpsimd
            def _(h):
                emit_engine("pool", h)

            @block.sync
            def _(h):
                emit_engine("sp", h)


D = 1024
D_IN = 8456
QA, KA, VA, GA = 0, 512, 1024, 1536
QB, KB, VB, GB = 2048, 2560, 2688, 2816
QC, KC, VC, FC, GC = 3328, 3840, 4352, 4864, 4872
GM = 5384
RMS_EPS = 1e-6
SUBLN_EPS = 1e-5
NEG = -30000.0
NAUG = 6
KROWS = 64 + NAUG


def alibi_tables(S):
    pos = np.arange(S)
    blk = (pos // 128).astype(np.float64)
    rem = (pos % 128).astype(np.float64)
    augA = np.zeros((4, 2, NAUG, S), np.float32)
    for h in range(4):
        m = 2.0 ** (-8.0 * (h + 1) / 4)
        augA[h, 0, 0] = -8 * m * 128 * blk
        augA[h, 0, 1] = -8 * m * rem
        augA[h, 0, 2] = 1.0
        augA[h, 0, 3] = 1.0
        augA[h, 1, 0] = 1.0
        augA[h, 1, 1] = 1.0
        augA[h, 1, 2] = 8 * m * 128 * blk
        augA[h, 1, 3] = 8 * m * rem
    augB = np.zeros((8, 2, NAUG, S), np.float32)
    for h in range(8):
        m = 2.0 ** (-8.0 * (h + 1) / 8)
        augB[h, 0, 0] = -8 * m * 128 * blk
        augB[h, 0, 1] = -8 * m * rem
        augB[h, 0, 2] = 8 * m
        augB[h, 0, 3] = 8 * m
        augB[h, 1, 0] = 1.0
        augB[h, 1, 1] = 1.0
        augB[h, 1, 2] = 128 * blk
        augB[h, 1, 3] = rem
    return augA, augB


def unit_desc(n, u):
    if n == 0:
        return dict(kind="A", q=[QA + u * 128, QA + u * 128 + 64], k=[KA + u * 128, KA + u * 128 + 64],
                    v=(VA + u * 128, 128), g=GA + u * 128)
    if n == 1:
        g = u // 2
        return dict(kind="B", q=[QB + (2 * u) * 64, QB + (2 * u + 1) * 64], k=[KB + g * 64, KB + g * 64],
                    v=(VB + g * 64, 64), g=GB + u * 128)
    return dict(kind="C", q=[QC + (2 * u) * 64, QC + (2 * u + 1) * 64], k=[KC + (2 * u) * 64, KC + (2 * u + 1) * 64],
                v=(VC + u * 128, 128), g=GC + u * 128)


def build(S=4096, depth=4):
    NB = S // 128
    NCH = S // 512
    nc = bass.Bass("TRN2", target_bir_lowering=False)

    def din(name, shape):
        return nc.dram_tensor(name, list(shape), F32, kind="ExternalInput").ap()

    x_d = din("x", [S, D])
    ng_d = din("norm_gain", [depth, D])
    win_d = din("w_in", [depth, D, D_IN])
    bf_d = din("b_forget", [depth, 8])
    lq1_d = din("lambda_q1", [depth, 64])
    lk1_d = din("lambda_k1", [depth, 64])
    lq2_d = din("lambda_q2", [depth, 64])
    lk2_d = din("lambda_k2", [depth, 64])
    sg_d = din("subln_gain", [depth, 128])
    sk_d = din("sinks", [depth, 8])
    wup_d = [din("w_up_a", [depth, 512, D]), din("w_up_b", [depth, 512, D]), din("w_up_c", [depth, 512, D])]
    wo_d = din("w_o", [depth, D, D])
    fg_d = din("final_gain", [D])
    augA_d = din("augA", [4, 2, NAUG, S])
    augB_d = din("augB", [8, 2, NAUG, S])
    out_d = nc.dram_tensor("out", [S, D], F32, kind="ExternalOutput").ap()
    xres_d = nc.dram_tensor("xres", [S, D], F32, kind="Internal").ap()
    M_d = nc.dram_tensor("mscr", [D, S], BF16, kind="Internal").ap()
    caug_d = nc.dram_tensor("caug", [2, 8, NAUG, S], BF16, kind="Internal").ap()

    sch = Sched(nc)
    bufs = {}

    def B(name):
        if name not in bufs:
            bufs[name] = Buf(name)
        return bufs[name]

    def sb(name, shape, dt):
        return nc.alloc_sbuf_tensor(name, list(shape), dt)

    hT = sb("hT", [128, 8, S], BF16)
    QT = sb("QT", [128, 2, S], BF16)
    KT = sb("KT", [128, 2, S], BF16)
    V = sb("V", [128, NB, 192], BF16)
    GT = sb("GT", [128, S], BF16)
    ygT = sb("ygT", [128, 4, S], BF16)
    wslab = [sb("wslab%d" % i, [128, 8, 512], BF16) for i in range(2)]
    pT = [sb("pT%d" % i, [128, 512], BF16) for i in range(4)]
    wup = sb("wup", [128, 4, D], BF16)
    wgm = sb("wgm", [128, 8, D], BF16)
    Wall = sb("Wall", [128, 6, 512], F32)
    W = [Wall[:, i, :] for i in range(6)]
    gain_bc = Wall[:, 4:6, :].rearrange("p a b -> p (a b)")
    Wb = None
    sgt = [W[i][:].bitcast(BF16)[:, 0:512] for i in range(2)]
    tst = [W[2 + i][:].bitcast(BF16)[:, 0:512] for i in range(2)]
    ysqb = sb("ysqb", [128, 512], BF16)
    onesbf = sb("onesbf", [128, 128], BF16)
    sgcol = sb("sgcol", [128, 1], F32)
    ident = sb("ident", [128, 128], BF16)
    maskD = sb("maskD", [128, 128], BF16)
    maskP = sb("maskP", [128, 128], BF16)
    lam4 = sb("lam4", [128, 4, 64], F32)
    lamw = sb("lamw", [128, 8], F32)
    neglam = sb("neglam", [128, 1], F32)
    esink = sb("esink", [128, 8], F32)
    negb = sb("negb", [8, 1], F32)
    wfc = sb("wfc", [128, 8, 8], BF16)
    rst = [sb("rst%d" % i, [128, 4], F32) for i in range(4)]
    yraw = ygT[:].rearrange("p a s -> p (a s)")

    def alias(off_bytes, shape, dt):
        n = int(np.prod(shape[1:]))
        esz = 4 if dt == F32 else 2
        a = yraw[:, off_bytes // 2: off_bytes // 2 + n * esz // 2]
        if dt == F32:
            a = a.bitcast(F32)
        if len(shape) == 3:
            a = a.rearrange("p (a b) -> p a b", a=shape[1])
        return a

    NXT = 4 if 8 * S >= 26624 else 2
    xt = [alias(4096 * i, [128, D], F32) for i in range(NXT)]
    hbt = [alias(4096 * NXT + 2048 * i, [128, D], BF16) for i in range(NXT)]
    junk = alias(6144 * NXT, [128, D], BF16)
    qraw = QT[:].rearrange("p a s -> p (a s)")
    kraw = KT[:].rearrange("p a s -> p (a s)")
    assert S >= 2048, "aliasing plan needs S >= 2048"
    big = 2 * S >= 8192
    wgraw = wgm[:].rearrange("p a b -> p (a b)")
    wuraw = wup[:].rearrange("p a b -> p (a b)")

    def tview(raw, off, shape, dt):
        n = int(np.prod(shape[1:]))
        esz = 4 if dt == F32 else 2
        a = raw[0:shape[0], off // 2: off // 2 + n * esz // 2]
        if dt == F32:
            a = a.bitcast(F32)
        if len(shape) == 3:
            a = a.rearrange("p (a b) -> p a b", a=shape[1])
        return a
    fcE = tview(wgraw, 0, [8, 512], F32)
    fcS = tview(wgraw, 2048, [8, 512], F32)
    fcC = [tview(wgraw, 4096, [8, 512], F32), tview(wgraw, 6144, [8, 512], F32)]
    ones8 = tview(wgraw, 8192, [8, 512], F32)
    caq = tview(wgraw, 10240, [8, NAUG, 512], BF16)
    cak = tview(wuraw, 0, [8, NAUG, 512], BF16)
    fcR, fcR2 = fcS, fcE
    mT = kraw[:, 4096:8192].rearrange("p (a b) -> p a b", a=8) if big else sb("mTx", [128, 8, 512], BF16)[:]
    mT2 = (V[:].rearrange("p a b -> p (a b)")[:, 0:4096].rearrange("p (a b) -> p a b", a=8) if big
           else sb("mT2x", [128, 8, 512], BF16)[:])

    ps = [nc.alloc_psum_tensor("ps%d" % i, [128, 512], F32) for i in range(8)]
    psB = [B("ps%d" % i) for i in range(8)]
    ST = (0, 1, 2)
    ACC = (3, 4, 5, 6)
    TR = 7

    hTb = [B("hT%d" % c) for c in range(NCH)]
    QTb = [B("QT0"), B("QT1")]
    KTb = [B("KT0"), B("KT1")]
    QTa = [B("QTa0"), B("QTa1")]
    KTa = [B("KTa0"), B("KTa1")]
    AL = [B("xt%d" % i) for i in range(4)] + [B("hbt%d" % i) for i in range(4)] + [B("junk")]
    mTb = [KTb[1], KTa[1]] if big else [B("mTx")]
    Vb = B("V")
    GTb = [B("GT%d" % c) for c in range(NCH)]
    Wb = [B("W%d" % i) for i in range(6)]
    mT2b = [Vb] if big else [B("mT2x")]
    ygTb = [[B("ygT%d_%d" % (u, c)) for c in range(NCH)] for u in range(4)]
    xres_b = [B("xres%d" % t) for t in range(NB)]
    M_b = [[B("M_%d_%d" % (c, dc)) for dc in range(8)] for c in range(NCH)]
    caug_b = [[B("caug%d_%d" % (sd, c)) for c in range(NCH)] for sd in range(2)]


    sch.op("pool", lambda h: h.memset(ident[:], 1.0), writes=[B("ident")])
    sch.op("pool", lambda h: h.affine_select(out=ident[:], in_=ident[:], pattern=[[-1, 128]],
                                             compare_op=ALU.is_equal, fill=0.0, base=0, channel_multiplier=1),
           reads=[B("ident")], writes=[B("ident")])
    sch.op("pool", lambda h: h.memset(maskD[:], 0.0), writes=[B("maskD")])
    sch.op("pool", lambda h: h.affine_select(out=maskD[:], in_=maskD[:], pattern=[[1, 128]],
                                             compare_op=ALU.is_ge, fill=NEG, base=0, channel_multiplier=-1),
           reads=[B("maskD")], writes=[B("maskD")])
    sch.op("pool", lambda h: h.memset(maskP[:], 0.0), writes=[B("maskP")])
    sch.op("pool", lambda h: h.affine_select(out=maskP[:], in_=maskP[:], pattern=[[-1, 128]],
                                             compare_op=ALU.is_gt, fill=NEG, base=0, channel_multiplier=1),
           reads=[B("maskP")], writes=[B("maskP")])
    sch.op("pool", lambda h: h.memset(onesbf[:], 1.0), writes=[B("onesbf")])

    tick_q = []

    def defer(fn, delay):
        tick_q.append([delay, fn])

    def tick():
        for it in tick_q:
            it[0] -= 1
        ready = [it for it in tick_q if it[0] <= 0]
        tick_q[:] = [it for it in tick_q if it[0] > 0]
        for it in ready:
            it[1]()

    def flush():
        while tick_q:
            tick_q.pop(0)[1]()

    wslab_b = [[B("wslab%d_%d" % (i, p)) for p in range(6)] for i in range(2)]
    slab_ctr = [0]

    def load_slab(l, ud):
        si = slab_ctr[0] % 2
        slab_ctr[0] += 1
        ws = wslab[si]

        def ld(part, dst0, c0, w):
            src = win_d[l, :, c0:c0 + w].rearrange("(kc p) j -> p kc j", p=128)
            sch.dma("pool", ws[:, :, dst0:dst0 + w], src, writes=[wslab_b[si][part]])
        ld(0, 0, ud["q"][0], 64)
        ld(1, 64, ud["q"][1], 64)
        ld(2, 128, ud["k"][0], 64)
        ld(3, 192, ud["k"][1], 64)
        ld(4, 256, ud["v"][0], ud["v"][1])
        ld(5, 384, ud["g"], 128)
        return si

    def load_aug(l, n, u, ud):
        for m in range(2):
            if ud["kind"] == "A":
                sch.dma("pool", QT[64:KROWS, m, :], augA_d[u, 0], writes=[QTa[m]])
                sch.dma("pool", KT[64:KROWS, m, :], augA_d[u, 1], writes=[KTa[m]])
            elif ud["kind"] == "B":
                sch.dma("pool", QT[64:KROWS, m, :], augB_d[2 * u + m, 0], writes=[QTa[m]])
                sch.dma("pool", KT[64:KROWS, m, :], augB_d[2 * u + m, 1], writes=[KTa[m]])
            else:
                sch.dma("sp", QT[64:KROWS, m, :], caug_d[0, 2 * u + m], reads=caug_b[0], writes=[QTa[m]])
                sch.dma("sp", KT[64:KROWS, m, :], caug_d[1, 2 * u + m], reads=caug_b[1], writes=[KTa[m]])

    def rms_to_h(xtile, xb, gbc, gb, slot, l):
        st = rst[slot]
        stb = B("rst%d" % slot)
        sch.op("act", lambda h: h.activation(out=junk, in_=xtile, func=AF.Square, accum_out=st[:, 0:1]),
               reads=[xb], writes=[B("junk"), stb])
        sch.op("act", lambda h: h.activation(out=st[:, 3:4], in_=st[:, 0:1], func=AF.Ln, scale=1.0 / D, bias=RMS_EPS),
               reads=[stb], writes=[stb])
        sch.op("act", lambda h: h.activation(out=st[:, 2:3], in_=st[:, 3:4], func=AF.Exp, scale=-0.5),
               reads=[stb], writes=[stb])
        return st, stb

    def prologue(l, on_tile=None):
        src = x_d if l == 0 else xres_d
        sch.dma("sp", gain_bc, ng_d[l].partition_broadcast(128), writes=[Wb[4], Wb[5]], owner=Wb[4])
        trbanks = (TR, ACC[0])

        def st_a(t):
            s = t % NXT
            rd = [xres_b[t]] if l > 0 else []
            sch.dma("sp", xt[s], src[t * 128:(t + 1) * 128, :], reads=rd, writes=[B("xt%d" % s)])

        def st_b(t):
            s = t % NXT
            st, stb, xb = rst[s], B("rst%d" % s), B("xt%d" % s)
            sch.op("act", lambda h: h.activation(out=junk, in_=xt[s], func=AF.Square, accum_out=st[:, 0:1]),
                   reads=[xb], writes=[B("junk"), stb])

        def st_c(t):
            s = t % NXT
            st, stb, xb, hb = rst[s], B("rst%d" % s), B("xt%d" % s), B("hbt%d" % s)
            sch.op("act", lambda h: h.activation(out=st[:, 3:4], in_=st[:, 0:1], func=AF.Ln, scale=1.0 / D, bias=RMS_EPS),
                   reads=[stb], writes=[stb])
            sch.op("act", lambda h: h.activation(out=st[:, 2:3], in_=st[:, 3:4], func=AF.Exp, scale=-0.5),
                   reads=[stb], writes=[stb])
            sch.op("dve", lambda h: h.scalar_tensor_tensor(
                out=hbt[s], in0=xt[s], scalar=st[:, 2:3], in1=gain_bc, op0=ALU.mult, op1=ALU.mult),
                reads=[xb, stb, Wb[4], Wb[5]], writes=[hb])
            bk = trbanks[t % 2]
            pst = ps[bk][:].bitcast(BF16)

            def tr(h):
                ins = None
                for kc in range(8):
                    ins = h.transpose(pst[:, kc * 128:(kc + 1) * 128], hbt[s][:, kc * 128:(kc + 1) * 128], ident[:])
                return ins
            sch.op("pe", tr, reads=[hb, B("ident")], writes=[psB[bk]])

        def st_d(t):
            bk = trbanks[t % 2]
            pst = ps[bk][:].bitcast(BF16)
            sch.op("dve", lambda h: h.tensor_copy(
                out=hT[:, :, t * 128:(t + 1) * 128], in_=pst.rearrange("p (a b) -> p a b", a=8)),
                reads=[psB[bk]], writes=[hTb[t // 4]])

        ob, oc = (1, 2) if NXT >= 4 else (0, 1)
        od = oc + 1
        for t in range(NB + od):
            if t < NB:
                st_a(t)
            if 0 <= t - ob < NB:
                st_b(t - ob)
            if 0 <= t - oc < NB:
                st_c(t - oc)
            if 0 <= t - od < NB:
                st_d(t - od)
                if on_tile is not None:
                    on_tile(t - od)

    def layer_scalars(l):
        lb = B("lam")
        for i, d_ in enumerate((lq1_d, lk1_d, lq2_d, lk2_d)):
            sch.dma("sp", lam4[:, i, :], d_[l].partition_broadcast(128), writes=[B("lam4_%d" % i)], owner=B("lam4_%d" % i))
        rl = [B("lam4_%d" % i) for i in range(4)]
        sch.op("dve", lambda h: h.tensor_tensor(out=lam4[:, 0, :], in0=lam4[:, 0, :], in1=lam4[:, 1, :], op=ALU.mult),
               reads=rl[:2], writes=[rl[0]])
        sch.op("dve", lambda h: h.tensor_tensor(out=lam4[:, 2, :], in0=lam4[:, 2, :], in1=lam4[:, 3, :], op=ALU.mult),
               reads=rl[2:], writes=[rl[2]])
        sch.op("dve", lambda h: h.reduce_sum(out=lamw[:, 0:1], in_=lam4[:, 0, :], axis=AX.X), reads=[rl[0]], writes=[lb])
        sch.op("dve", lambda h: h.reduce_sum(out=lamw[:, 1:2], in_=lam4[:, 2, :], axis=AX.X), reads=[rl[2], lb], writes=[lb])
        sch.op("act", lambda h: h.activation(out=lamw[:, 2:4], in_=lamw[:, 0:2], func=AF.Exp), reads=[lb], writes=[lb])
        lam_init = 0.8 - 0.6 * math.exp(-0.3 * l)
        sch.op("dve", lambda h: h.tensor_tensor(out=lamw[:, 4:5], in0=lamw[:, 3:4], in1=lamw[:, 2:3], op=ALU.subtract),
               reads=[lb], writes=[lb])
        sch.op("dve", lambda h: h.tensor_scalar(out=neglam[:], in0=lamw[:, 4:5], scalar1=-lam_init, scalar2=None,
                                                op0=ALU.add), reads=[lb], writes=[B("neglam")])
        sch.dma("sp", sgcol[:], sg_d[l].rearrange("(a b) -> a b", b=1), writes=[B("sgcol")])
        sch.op("dve", lambda h: h.tensor_scalar(out=sgcol[:], in0=sgcol[:], scalar1=(1.0 - lam_init) * math.sqrt(128.0),
                                                scalar2=None, op0=ALU.mult), reads=[B("sgcol")], writes=[B("sgcol")])
        sch.dma("sp", esink[:], sk_d[l].partition_broadcast(128), writes=[B("esink")])
        sch.op("act", lambda h: h.activation(out=esink[:], in_=esink[:], func=AF.Exp), reads=[B("esink")], writes=[B("esink")])
        sch.dma("sp", negb[:], bf_d[l].rearrange("(a b) -> a b", b=1), writes=[B("negb")])
        sch.op("dve", lambda h: h.tensor_scalar(out=negb[:], in0=negb[:], scalar1=-1.0, scalar2=None, op0=ALU.mult),
               reads=[B("negb")], writes=[B("negb")])

    def forget_gates(l):
        wb = B("wfc")
        sch.dma("pool", wfc[:], win_d[l, :, FC:FC + 8].rearrange("(kc p) j -> p kc j", p=128), writes=[wb])
        FGT = [B(nm) for nm in ("fcE", "fcS", "fcC0", "fcC1", "ones8", "caq", "cak")]
        FG = [B("wgm"), B("wup")]
        state = {"prev": None}
        qb_, kb_ = B("caq"), B("cak")

        def entry():
            sch.op("pool", lambda h: h.memset(ones8, 1.0), writes=FG + FGT)
            sch.op("pool", lambda h: h.memset(caq, 1.0), reads=[B("ones8")], writes=[qb_])
            sch.op("pool", lambda h: h.memset(cak, 1.0), reads=[B("ones8")], writes=[kb_])

        def stage1(c):
            def mm(h):
                ins = None
                for kc in range(8):
                    ins = h.matmul(ps[TR][0:8, :], lhsT=wfc[:, kc, :], rhs=hT[:, kc, c * 512:(c + 1) * 512],
                                   start=(kc == 0), stop=(kc == 7))
                return ins
            sch.op("pe", mm, reads=[wb, hTb[c]], writes=[psB[TR]])
            sch.op("act", lambda h: h.activation(out=fcE[:], in_=ps[TR][0:8, :], func=AF.Exp,
                                                 bias=negb[:, 0:1], scale=-1.0),
                   reads=[psB[TR], B("negb")], writes=[B("fcE")])
            sch.op("act", lambda h: h.activation(out=fcS[:], in_=fcE[:], func=AF.Ln, bias=1.0, scale=1.0),
                   reads=[B("fcE")], writes=[B("fcS")])

        def stage2(c):
            sch.op("dve", lambda h: h.tensor_scalar(out=fcS[:], in0=fcS[:], scalar1=-8.0, scalar2=None, op0=ALU.mult),
                   reads=[B("fcS")], writes=[B("fcS")])
            cur = fcC[c % 2]
            curb = B("fcC%d" % (c % 2))
            prev = state["prev"]
            init = 0.0 if prev is None else prev[0][:, 511:512]
            rd = [B("ones8"), B("fcS")] + ([prev[1]] if prev is not None else [])
            sch.op("dve", lambda h: h.tensor_tensor_scan(
                out=cur[:], data0=ones8[:], data1=fcS[:], initial=init, op0=ALU.mult, op1=ALU.add),
                reads=rd, writes=[curb])
            state["prev"] = (cur, curb)
            sch.op("dve", lambda h: h.tensor_copy(out=caq[:, 0, :], in_=cur[:]), reads=[curb], writes=[qb_])
            sch.op("dve", lambda h: h.tensor_tensor(out=fcR[:], in0=cur[:], in1=caq[:, 0, :], op=ALU.subtract),
                   reads=[curb, qb_], writes=[B("fcS")])
            sch.op("dve", lambda h: h.tensor_copy(out=caq[:, 1, :], in_=fcR[:]), reads=[B("fcS")], writes=[qb_])
            sch.op("dve", lambda h: h.tensor_tensor(out=fcR2[:], in0=fcR[:], in1=caq[:, 1, :], op=ALU.subtract),
                   reads=[B("fcS"), qb_], writes=[B("fcE")])
            sch.op("dve", lambda h: h.tensor_copy(out=caq[:, 2, :], in_=fcR2[:]), reads=[B("fcE")], writes=[qb_])
            sch.op("dve", lambda h: h.tensor_scalar(out=cak[:, 3:6, :], in0=caq[:, 0:3, :], scalar1=-1.0, scalar2=None,
                                                    op0=ALU.mult), reads=[qb_], writes=[kb_])
            sch.dma("sp", caug_d[0, :, :, c * 512:(c + 1) * 512], caq[:], reads=[qb_], writes=[caug_b[0][c]], owner=qb_)
            sch.dma("sp", caug_d[1, :, :, c * 512:(c + 1) * 512], cak[:], reads=[kb_], writes=[caug_b[1][c]], owner=kb_)

        def exit_guard():
            sch.op("pool", lambda h: h.memset(ones8[:, 0:1], 1.0), writes=FG + FGT)

        defer(entry, 1)
        for c in range(NCH):
            defer(lambda c=c: stage1(c), 2 + 2 * c)
            defer(lambda c=c: stage2(c), 3 + 2 * c)
        defer(exit_guard, 4 + 2 * NCH)

    proj_ctr = [0]

    def proj_chunk(ud, si, c):
        for st_ in proj_steps(ud, si, c):
            st_()

    def proj_steps(ud, si, c, later=None):
        return [lambda: proj_qk(ud, si, c, 0, later), lambda: proj_qk(ud, si, c, 1, later),
                lambda: proj_v(ud, si, c, later), lambda: proj_g(ud, si, c, later)]

    def _evac(fn, later):
        if later is None:
            fn()
        else:
            later.append(fn)

    def proj_qk(ud, si, c, which, later=None):
        ws = wslab[si]
        wsb = wslab_b[si]
        kind = ud["kind"]
        for (dstT, dstb, c0, eng) in (((QT, QTb, 0, "dve"), (KT, KTb, 128, "act"))[which],):
            bank = ST[proj_ctr[0] % 3]
            proj_ctr[0] += 1

            def mm(h, bank=bank, c0=c0):
                ins = None
                for kc in range(8):
                    ins = h.matmul(ps[bank][:, :], lhsT=ws[:, kc, c0:c0 + 128],
                                   rhs=hT[:, kc, c * 512:(c + 1) * 512], start=(kc == 0), stop=(kc == 7))
                return ins
            sch.op("pe", mm, reads=wsb + [hTb[c]], writes=[psB[bank]])

            def ev(bank=bank, dstT=dstT, dstb=dstb, eng=eng):
                for m in range(2):
                    if eng == "dve":
                        sch.op("dve", lambda h, m=m: h.tensor_copy(
                            out=dstT[0:64, m, c * 512:(c + 1) * 512], in_=ps[bank][m * 64:(m + 1) * 64, :]),
                            reads=[psB[bank]], writes=[dstb[m]])
                    else:
                        sch.op("act", lambda h, m=m: h.activation(
                            out=dstT[0:64, m, c * 512:(c + 1) * 512], in_=ps[bank][m * 64:(m + 1) * 64, :], func=AF.Copy),
                            reads=[psB[bank]], writes=[dstb[m]])
            _evac(ev, later)
            tick()

    def proj_v(ud, si, c, later=None):
        ws = wslab[si]
        wsb = wslab_b[si]
        kind = ud["kind"]
        vw = ud["v"][1]
        t4 = c
        bank = ST[proj_ctr[0] % 3]
        proj_ctr[0] += 1

        def mmv(h):
            ins = None
            for tt in range(4):
                t = 4 * t4 + tt
                for kc in range(8):
                    ins = h.matmul(ps[bank][:, tt * 128:tt * 128 + vw], lhsT=hT[:, kc, t * 128:(t + 1) * 128],
                                   rhs=ws[:, kc, 256:256 + vw], start=(kc == 0), stop=(kc == 7))
            return ins
        sch.op("pe", mmv, reads=wsb + [hTb[t4]], writes=[psB[bank]])
        pv4 = ps[bank][:].rearrange("p (a b) -> p a b", a=4)
        blk = slice(4 * t4, 4 * t4 + 4)
        if kind == "A":
            moves = [((0, 128), (0, 128))]
        elif kind == "B":
            moves = [((0, 64), (0, 64)), ((128, 192), (0, 64))]
        else:
            moves = [((0, 64), (0, 64)), ((128, 192), (64, 128))]
        def ev():
            for (d0, d1), (s0, s1) in moves:
                sch.op("act", lambda h, d0=d0, d1=d1, s0=s0, s1=s1: h.activation(
                    out=V[:, blk, d0:d1], in_=pv4[:, :, s0:s1], func=AF.Copy), reads=[psB[bank]], writes=[Vb])
        _evac(ev, later)
        tick()

    def proj_g(ud, si, c, later=None):
        ws = wslab[si]
        wsb = wslab_b[si]
        bank2 = ST[proj_ctr[0] % 3]
        proj_ctr[0] += 1

        def mmg(h):
            ins = None
            for kc in range(8):
                ins = h.matmul(ps[bank2][:, :], lhsT=ws[:, kc, 384:512], rhs=hT[:, kc, c * 512:(c + 1) * 512],
                               start=(kc == 0), stop=(kc == 7))
            return ins
        sch.op("pe", mmg, reads=wsb + [hTb[c]], writes=[psB[bank2]])
        def ev():
            sch.op("act", lambda h: h.activation(out=GT[:, c * 512:(c + 1) * 512], in_=ps[bank2][:, :],
                                                 func=AF.Silu), reads=[psB[bank2]], writes=[GTb[c]])
        _evac(ev, later)
        tick()

    def unit_projection(l, n, u, ud, si):
        if ud["kind"] == "B" and u == 0:
            sch.op("pool", lambda h: h.memset(V[:, :, 64:128], 1.0), writes=[Vb])
        reuse_kv = ud["kind"] == "B" and u % 2 == 1
        for c in range(NCH):
            for k_, st_ in enumerate(proj_steps(ud, si, c)):
                if reuse_kv and k_ in (1, 2):
                    continue
                st_()

    def attention(l, n, u, ud):
        kind = ud["kind"]
        ticks = []
        for c in range(NCH):
            for m in range(2):
                js = [j for j in range(4 * c - 1, 4 * c + 4) if j >= 0] if kind == "B" else list(range(4 * c + 4))
                for idx, j in enumerate(js):
                    lo = max(j - 4 * c, 0)
                    hi = min(j - 4 * c + 2, 4) if kind == "B" else 4
                    ticks.append((c, m, j, lo, hi, idx == 0, idx == len(js) - 1))
        NT = len(ticks)
        LAT = 3

        def acc_banks(c, m):
            if kind == "A":
                return ACC[2 * m], ACC[2 * m + 1]
            return ACC[(2 * c + m) % 4], None

        def qk(n_):
            c, m, j, lo, hi, first, last = ticks[n_]
            bank = ST[n_ % 3]
            masks = []
            if j >= 4 * c:
                masks.append((j - 4 * c, maskD))
            if kind == "B" and j - 4 * c + 1 < 4:
                masks.append((j - 4 * c + 1, maskP))

            def mm(h):
                ins = h.matmul(ps[bank][:, lo * 128:hi * 128], lhsT=KT[0:KROWS, m, j * 128:(j + 1) * 128],
                               rhs=QT[0:KROWS, m, (4 * c + lo) * 128:(4 * c + hi) * 128], start=True, stop=(not masks))
                for k_, (qq, msk) in enumerate(masks):
                    ins = h.matmul(ps[bank][:, qq * 128:(qq + 1) * 128], lhsT=ident[:], rhs=msk[:], start=False,
                                   stop=(k_ == len(masks) - 1))
                return ins
            sch.op("pe", mm, reads=[KTb[m], QTb[m], KTa[m], QTa[m], B("ident"), B("maskD"), B("maskP")], writes=[psB[bank]])
            sch.op("act", lambda h: h.activation(out=pT[n_ % 4][:, lo * 128:hi * 128], in_=ps[bank][:, lo * 128:hi * 128],
                                                 func=AF.Exp, scale=0.125),
                   reads=[psB[bank]], writes=[B("pT%d" % (n_ % 4))])

        def pv(n_):
            c, m, j, lo, hi, first, last = ticks[n_]
            by, bl = acc_banks(c, m)
            p_ = pT[n_ % 4][:, lo * 128:hi * 128]
            if kind == "A":
                def mm(h):
                    h.matmul(ps[by][:, lo * 128:hi * 128], lhsT=V[:, j, 0:128], rhs=p_, start=first, stop=last)
                    return h.matmul(ps[bl][:, lo * 128:hi * 128], lhsT=onesbf[:], rhs=p_, start=first, stop=last)
                sch.op("pe", mm, reads=[B("pT%d" % (n_ % 4)), Vb, B("onesbf")], writes=[psB[by], psB[bl]])
            else:
                vs = V[:, j, 0:128] if m == 0 else V[:, j, 64:192]

                def mm(h):
                    return h.matmul(ps[by][:, lo * 128:hi * 128], lhsT=vs, rhs=p_, start=first, stop=last,
                                    skip_group_check=(kind == "B"))
                sch.op("pe", mm, reads=[B("pT%d" % (n_ % 4)), Vb], writes=[psB[by]])
            if last:
                finalize(c, m, by, bl)

        def finalize(c, m, by, bl):
            cs = slice(c * 512, (c + 1) * 512)
            outb = [ygTb[u][c]] + AL
            def recip_l(wi, bl_):
                if True:
                    sch.op("act", lambda h: h.activation(out=W[wi][:], in_=ps[bl_][:, :], func=AF.Ln),
                           reads=[psB[bl_]], writes=[Wb[wi]])
                    sch.op("act", lambda h: h.activation(out=W[wi][:], in_=W[wi][:], func=AF.Exp, scale=-1.0),
                           reads=[Wb[wi]], writes=[Wb[wi]])
                else:
                    sch.op("dve", lambda h: h.reciprocal(out=W[wi][:], in_=ps[bl_][:, :]), reads=[psB[bl_]], writes=[Wb[wi]])
            if kind == "A":
                if m == 0:
                    recip_l(0, bl)
                    sch.op("dve", lambda h: h.tensor_tensor(out=W[2][:], in0=ps[by][:, :], in1=W[0][:], op=ALU.mult),
                           reads=[psB[by], Wb[0]], writes=[Wb[2]])
                    return
                recip_l(1, bl)
                sch.op("dve", lambda h: h.tensor_tensor(out=W[3][:], in0=ps[by][:, :], in1=W[1][:], op=ALU.mult),
                       reads=[psB[by], Wb[1]], writes=[Wb[3]])
                sch.op("dve", lambda h: h.scalar_tensor_tensor(out=W[4][:], in0=W[3][:], scalar=neglam[:, 0:1], in1=W[2][:],
                                                                op0=ALU.mult, op1=ALU.add),
                       reads=[Wb[3], Wb[2], B("neglam")], writes=[Wb[4]])
                sch.op("dve", lambda h: h.tensor_tensor(out=ysqb[:], in0=W[4][:], in1=W[4][:], op=ALU.mult),
                       reads=[Wb[4]], writes=[B("ysqb")])

                def stage2():
                    sch.op("pe", lambda h: h.matmul(ps[TR][:, :], lhsT=onesbf[:], rhs=ysqb[:], start=True, stop=True),
                           reads=[B("onesbf"), B("ysqb")], writes=[psB[TR]])
                    sch.op("act", lambda h: h.activation(out=W[5][:], in_=ps[TR][:, :], func=AF.Ln, bias=128.0 * SUBLN_EPS),
                           reads=[psB[TR]], writes=[Wb[5]])
                    sch.op("act", lambda h: h.activation(out=W[5][:], in_=W[5][:], func=AF.Exp, scale=-0.5),
                           reads=[Wb[5]], writes=[Wb[5]])
                    sch.op("dve", lambda h: h.tensor_tensor(out=W[4][:], in0=W[4][:], in1=W[5][:], op=ALU.mult),
                           reads=[Wb[4], Wb[5]], writes=[Wb[4]])
                    sch.op("dve", lambda h: h.scalar_tensor_tensor(out=ygT[:, u, cs], in0=W[4][:], scalar=sgcol[:, 0:1],
                                                                    in1=GT[:, cs], op0=ALU.mult, op1=ALU.mult),
                           reads=[Wb[4], GTb[c], B("sgcol")], writes=outb)
                defer(stage2, 8)
                return
            y0, y1 = (0, 64) if m == 0 else (64, 128)
            l0, l1 = (64, 128) if m == 0 else (0, 64)
            rl, tmp = W[m], W[2 + m]
            rlb, tmpb = Wb[m], Wb[2 + m]
            if kind == "B":
                hh = 2 * u + m
                sch.op("act", lambda h: h.activation(out=rl[y0:y1, :], in_=ps[by][l0:l1, :], func=AF.Ln,
                                                     bias=esink[l0:l1, hh:hh + 1]),
                       reads=[psB[by], B("esink")], writes=[rlb])
            elif c < 2:
                sch.op("act", lambda h: h.activation(out=rl[y0:y1, :], in_=ps[by][l0:l1, :], func=AF.Ln),
                       reads=[psB[by]], writes=[rlb])
            if kind == "B" or c < 2:
                sch.op("act", lambda h: h.activation(out=rl[y0:y1, :], in_=rl[y0:y1, :], func=AF.Exp, scale=-1.0),
                       reads=[rlb], writes=[rlb])
            else:
                sch.op("dve", lambda h: h.reciprocal(out=rl[y0:y1, :], in_=ps[by][l0:l1, :]), reads=[psB[by]], writes=[rlb])
            sch.op("dve", lambda h: h.tensor_tensor(out=tmp[y0:y1, :], in0=ps[by][y0:y1, :], in1=rl[y0:y1, :], op=ALU.mult),
                   reads=[psB[by], rlb], writes=[tmpb])
            sch.op("dve", lambda h: h.tensor_tensor(out=ygT[y0:y1, u, cs], in0=tmp[y0:y1, :], in1=GT[y0:y1, cs], op=ALU.mult),
                   reads=[tmpb, GTb[c]], writes=outb)

        for n_ in range(NT + LAT):
            if n_ < NT:
                qk(n_)
            if n_ >= LAT:
                pv(n_ - LAT)
            tick()

    def prefetch_epi(l, n):
        sch.dma("pool", wup[:], wup_d[n][l].rearrange("(ec p) j -> p ec j", p=128), writes=[B("wup")])
        sch.dma("pool", wgm[:], win_d[l, :, GM + n * D:GM + (n + 1) * D].rearrange("(kc p) j -> p kc j", p=128),
                writes=[B("wgm")])

    wo_t = [wslab[i][:].rearrange("p a b -> p (a b)").rearrange("p (k d) -> p k d", k=4) for i in range(2)]
    wo_b = wslab_b[0] + wslab_b[1]

    def prefetch_wo(l):
        for i in range(2):
            sch.dma("pool", wo_t[i], wo_d[l, i * 512:(i + 1) * 512, :].rearrange("(k p) d -> p k d", p=128),
                    writes=wslab_b[i])

    mTs = [(mT, mTb), (mT2, mT2b)]

    def load_mt(c):
        mTc, mTcb = mTs[c % 2]
        sch.dma("sp", mTc, M_d[:, c * 512:(c + 1) * 512].rearrange("(dc p) t -> p dc t", p=128),
                reads=M_b[c], writes=mTcb)

    def epilogue(l, n):
        prv = [W[4][:].bitcast(BF16)[:, 0:512], W[5][:].bitcast(BF16)[:, 0:512]]
        steps = [(c, dc) for c in range(NCH) for dc in range(8)]

        def load_prev(k):
            c, dc = steps[k]
            sch.dma("sp", prv[k % 2], M_d[dc * 128:(dc + 1) * 128, c * 512:(c + 1) * 512],
                    reads=[M_b[c][dc]], writes=[Wb[4 + k % 2]])
        if n > 0:
            load_prev(0)
        for k, (c, dc) in enumerate(steps):
            bu = ACC[(2 * k) % 4]
            bg = ACC[(2 * k + 1) % 4]
            s = k % 2
            if n > 0 and k + 1 < len(steps):
                load_prev(k + 1)
            if n == 2 and k in (12, 20) and NCH >= 4:
                load_mt((k - 12) // 8)

            def mmu(h, c=c, dc=dc, bu=bu):
                ins = None
                for ec in range(4):
                    ins = h.matmul(ps[bu][:], lhsT=wup[:, ec, dc * 128:(dc + 1) * 128],
                                   rhs=ygT[:, ec, c * 512:(c + 1) * 512], start=(ec == 0), stop=(ec == 3))
                return ins
            sch.op("pe", mmu, reads=[B("wup")] + [ygTb[e][c] for e in range(4)] + AL, writes=[psB[bu]])

            def mmg(h, c=c, dc=dc, bg=bg):
                ins = None
                for kc in range(8):
                    ins = h.matmul(ps[bg][:], lhsT=wgm[:, kc, dc * 128:(dc + 1) * 128],
                                   rhs=hT[:, kc, c * 512:(c + 1) * 512], start=(kc == 0), stop=(kc == 7))
                return ins
            sch.op("pe", mmg, reads=[B("wgm"), hTb[c]], writes=[psB[bg]])
            sch.op("act", lambda h, bg=bg, s=s: h.activation(out=sgt[s], in_=ps[bg][:], func=AF.Sigmoid),
                   reads=[psB[bg]], writes=[Wb[s]])
            sch.op("dve", lambda h, bu=bu, s=s: h.tensor_tensor(out=tst[s], in0=ps[bu][:], in1=sgt[s], op=ALU.mult),
                   reads=[psB[bu], Wb[s]], writes=[Wb[2 + s]])
            if n > 0:
                sch.op("dve", lambda h, s=s: h.tensor_tensor(out=tst[s], in0=tst[s], in1=prv[s], op=ALU.add),
                       reads=[Wb[2 + s], Wb[4 + s]], writes=[Wb[2 + s]])
            sch.dma("sp", M_d[dc * 128:(dc + 1) * 128, c * 512:(c + 1) * 512], tst[s],
                    reads=[Wb[2 + s]], writes=[M_b[c][dc]], owner=Wb[2 + s])
            tick()

    final_stores = []

    def final_phase(l):
        last = (l == depth - 1)
        if last:
            sch.dma("sp", gain_bc, fg_d.partition_broadcast(128), writes=[Wb[4], Wb[5]], owner=Wb[4])
        src = x_d if l == 0 else xres_d
        dst = out_d if last else xres_d
        pre = 2 if NCH >= 4 else 0
        if pre == 0:
            load_mt(0)

        def xload(t):
            rd = [xres_b[t]] if l > 0 else []
            sch.dma("sp", xt[t % NXT], src[t * 128:(t + 1) * 128, :], reads=rd, writes=[B("xt%d" % (t % NXT))])
        for c in range(NCH):
            mTc, mTcb = mTs[c % 2]
            if c + 1 < NCH and c + 1 >= pre:
                load_mt(c + 1)
            for tb in range(4):
                t = 4 * c + tb
                s = t % NXT
                xb = B("xt%d" % s)
                la = NXT - 2
                if t == 0:
                    for t_ in range(la):
                        xload(t_)
                if t + la < NB:
                    xload(t + la)
                for dh in range(2):
                    bank = ACC[(2 * t + dh) % 4]

                    def mm(h, tb=tb, dh=dh, bank=bank, mTc=mTc):
                        ins = None
                        for kc in range(8):
                            ins = h.matmul(ps[bank][:], lhsT=mTc[:, kc, tb * 128:(tb + 1) * 128],
                                           rhs=wo_t[kc // 4][:, kc % 4, dh * 512:(dh + 1) * 512],
                                           start=(kc == 0), stop=(kc == 7))
                        return ins
                    sch.op("pe", mm, reads=mTcb + wo_b, writes=[psB[bank]])
                    sch.op("dve", lambda h, s=s, dh=dh, bank=bank: h.tensor_tensor(
                        out=xt[s][:, dh * 512:(dh + 1) * 512], in0=ps[bank][:], in1=xt[s][:, dh * 512:(dh + 1) * 512],
                        op=ALU.add), reads=[psB[bank], xb], writes=[xb])
                if last:
                    st, stb = rms_to_h(xt[s], xb, gain_bc, Wb[4], s, l)
                    sch.op("dve", lambda h, s=s, st=st: h.scalar_tensor_tensor(
                        out=xt[s], in0=xt[s], scalar=st[:, 2:3], in1=gain_bc, op0=ALU.mult, op1=ALU.mult),
                        reads=[xb, stb, Wb[4], Wb[5]], writes=[xb])
                d = sch.dma("sp", dst[t * 128:(t + 1) * 128, :], xt[s], reads=[xb], writes=[xres_b[t]], owner=xb)
                if last:
                    final_stores.append(d)
                tick()

    for l in range(depth):
        layer_scalars(l)
        units = [(n, u, unit_desc(n, u)) for n in range(3) for u in range(4)]
        si_next = load_slab(l, units[0][2])
        load_aug(l, 0, 0, units[0][2])
        pend = []

        def on_tile(t, si=si_next, ud=units[0][2]):
            c, k_ = t // 4, t % 4
            evs = pend[:]
            del pend[:]
            for ev in evs:
                ev()
            if c >= 1:
                proj_steps(ud, si, c - 1, pend)[k_]()
        prologue(l, on_tile=on_tile)
        for ev in pend:
            ev()
        proj_chunk(units[0][2], si_next, NCH - 1)
        forget_gates(l)
        for idx, (n, u, ud) in enumerate(units):
            si = si_next
            if idx + 1 < len(units):
                si_next = load_slab(l, units[idx + 1][2])
            if idx > 0:
                load_aug(l, n, u, ud)
            if u == 2:
                prefetch_epi(l, n)
            if idx > 0:
                unit_projection(l, n, u, ud, si)
            if idx == len(units) - 1:
                prefetch_wo(l)
            attention(l, n, u, ud)
            if u == 3:
                flush()
                epilogue(l, n)
        flush()
        final_phase(l)
    sch.emit(final_wait_ops=final_stores)
    return nc, sch


_CACHE = {}


def kernel(**inputs):
    S, depth = 4096, 4
    x = np.ascontiguousarray(np.asarray(inputs["x"], dtype=np.float32))
    n = x.shape[0]
    if "nc" not in _CACHE:
        _CACHE["nc"] = build(S, depth)[0]
        _CACHE["aug"] = alibi_tables(S)
    nc = _CACHE["nc"]
    augA, augB = _CACHE["aug"]
    shared = {k: np.ascontiguousarray(np.asarray(inputs[k], dtype=np.float32)) for k in (
        "norm_gain", "w_in", "b_forget", "lambda_q1", "lambda_k1", "lambda_q2", "lambda_k2", "subln_gain",
        "sinks", "w_up_a", "w_up_b", "w_up_c", "w_o", "final_gain")}
    shared["augA"] = augA
    shared["augB"] = augB
    in_maps = [dict(shared, x=x[i]) for i in range(n)]
    res = run_bass_kernel_spmd(nc, in_maps, core_ids=list(range(n)))
    return np.stack([r["out"] for r in res.results], axis=0)
```
